# Optimizing a Trainium2 kernel written in Bass

```python
import math
import jax, jax.numpy as jnp
from jax import lax
import numpy as np

D_MODEL = 2048
BATCH = 4
SEQ = 2048
DEPTH = 4

N_EVEN_LAYERS = (DEPTH + 1) // 2
N_ODD_LAYERS = DEPTH // 2
RMS_EPS = 1e-6

SB_HEAD_DIM = 128
SB_HEADS = D_MODEL // 256
SB_WIDTH = SB_HEADS * SB_HEAD_DIM
SB_BLOCK = 128

GLA_HEADS = 4
GLA_HEAD_V = D_MODEL // (2 * GLA_HEADS)
GLA_HEAD_K = GLA_HEAD_V // 2
GLA_K_WIDTH = GLA_HEADS * GLA_HEAD_K
GLA_V_WIDTH = GLA_HEADS * GLA_HEAD_V
GLA_GATE_RANK = 16
GLA_GATE_NORMALIZER = 16.0
GLA_CHUNK = 64

AB_SPLIT = (SB_WIDTH, SB_WIDTH, SB_WIDTH, GLA_K_WIDTH, GLA_K_WIDTH, GLA_V_WIDTH, GLA_V_WIDTH, GLA_GATE_RANK)
AB_IN_COLS = sum(AB_SPLIT)
AB_OUT_WIDTH = SB_WIDTH + GLA_V_WIDTH

GDN_HEAD_DIM = 128
GDN_K_HEADS = D_MODEL // 128
GDN_V_HEADS = 2 * GDN_K_HEADS
GDN_K_WIDTH = GDN_K_HEADS * GDN_HEAD_DIM
GDN_V_WIDTH = GDN_V_HEADS * GDN_HEAD_DIM
GDN_CONV_K = 4
GDN_CONV_DIM = 2 * GDN_K_WIDTH + GDN_V_WIDTH
GDN_SPLIT = (GDN_CONV_DIM, GDN_V_WIDTH, GDN_V_HEADS, GDN_V_HEADS)
GDN_IN_COLS = sum(GDN_SPLIT)
GDN_CHUNK = 64
DT_MIN = 1e-3
DT_MAX = 1e-1

MEM_TOKENS = 256
MEM_HEADS = 4
MEM_HEAD_DIM = D_MODEL // MEM_HEADS

FFN_HIDDEN = -(-8 * D_MODEL // (3 * 256)) * 256

kernel_name = 'hybrid_stickbreak_gla_gdeltanet_memxattn'


def _split(x, sizes):
    offsets = [int(o) for o in np.cumsum(sizes)[:-1]]
    return jnp.split(x, offsets, axis=-1)


def rms_norm(x, g):
    xf = x.astype(jnp.float32)
    y = xf * lax.rsqrt(jnp.mean(xf * xf, axis=-1, keepdims=True) + RMS_EPS)
    return (y * g.astype(jnp.float32)).astype(x.dtype)


def l2_normalize(x):
    return x * lax.rsqrt(jnp.sum(x * x, axis=-1, keepdims=True) + RMS_EPS)


def stick_breaking_attention(q, k, v):
    T, d = q.shape[2], q.shape[3]
    scale = d ** -0.5
    outs = []
    for blk in range(T // SB_BLOCK):
        start = blk * SB_BLOCK
        stop = start + SB_BLOCK
        z = jnp.einsum('bhqd,bhkd->bhqk', q[:, :, start:stop], k[:, :, :stop]).astype(jnp.float32) * scale
        valid = jnp.arange(stop)[None, :] < (start + jnp.arange(SB_BLOCK))[:, None]
        log_keep = jnp.where(valid, jax.nn.log_sigmoid(-z), 0.0)
        log_after = lax.cumsum(log_keep, axis=3, reverse=True) - log_keep
        w = jnp.where(valid, jnp.exp(jax.nn.log_sigmoid(z) + log_after), 0.0)
        outs.append(jnp.einsum('bhqk,bhkd->bhqd', w.astype(v.dtype), v[:, :, :stop]))
    return jnp.concatenate(outs, axis=2)


def gla_chunked(q, k, v, log_a):
    B, H, T, dk = q.shape
    dv = v.shape[-1]
    C = GLA_CHUNK
    N = T // C
    q = (q * dk ** -0.5).reshape(B, H, N, C, dk)
    k = k.reshape(B, H, N, C, dk)
    v = v.reshape(B, H, N, C, dv)
    G = jnp.cumsum(log_a.reshape(B, H, N, C, dk), axis=3)
    G_last = G[:, :, :, -1:]
    q_dec = q * jnp.exp(G)
    scores = jnp.einsum('bhncd,bhnsd->bhncs', q_dec, k * jnp.exp(-G))
    causal = jnp.tril(jnp.ones((C, C), dtype=bool))
    o_intra = jnp.einsum('bhncs,bhnsv->bhncv', jnp.where(causal, scores, 0.0), v)
    dS = jnp.einsum('bhncd,bhncv->bhndv', k * jnp.exp(G_last - G), v)
    chunk_decay = jnp.exp(G_last[:, :, :, 0])

    def step(S, inp):
        dec, ds = inp
        return S * dec[..., None] + ds, S

    S0 = jnp.zeros((B, H, dk, dv), q.dtype)
    _, S_prev = lax.scan(step, S0, (jnp.moveaxis(chunk_decay, 2, 0), jnp.moveaxis(dS, 2, 0)))
    o_inter = jnp.einsum('bhncd,nbhdv->bhncv', q_dec, S_prev)
    return (o_intra + o_inter).reshape(B, H, T, dv)


def gated_delta_rule_chunked(q, k, v, beta, g):
    B, H, T, dk = q.shape
    dv = v.shape[-1]
    C = GDN_CHUNK
    N = T // C
    q = (q * dk ** -0.5).reshape(B, H, N, C, dk)
    k = k.reshape(B, H, N, C, dk)
    v = v.reshape(B, H, N, C, dv)
    beta = beta.reshape(B, H, N, C, 1)
    g = jnp.cumsum(g.reshape(B, H, N, C), axis=3)
    incl = jnp.tril(jnp.ones((C, C), dtype=bool))
    strict = jnp.tril(jnp.ones((C, C), dtype=bool), -1)
    diff = g[..., :, None] - g[..., None, :]
    decay = jnp.where(incl, jnp.exp(jnp.where(incl, diff, 0.0)), 0.0)
    kb = k * beta
    L = jnp.where(strict, jnp.einsum('bhncd,bhnsd->bhncs', kb, k) * decay, 0.0)
    eye = jnp.eye(C, dtype=q.dtype)
    rhs = jnp.concatenate([v * beta, kb * jnp.exp(g)[..., None]], axis=-1)
    sol = lax.linalg.triangular_solve(L + eye, rhs, left_side=True, lower=True, unit_diagonal=True)
    u, w = sol[..., :dv], sol[..., dv:]
    qk = jnp.einsum('bhncd,bhnsd->bhncs', q, k) * decay
    q_dec = q * jnp.exp(g)[..., None]
    g_last = g[..., -1]
    k_dec = k * jnp.exp(g_last[..., None] - g)[..., None]

    def step(S, inp):
        u_n, w_n, qk_n, qd_n, kd_n, gl_n = inp
        v_new = u_n - jnp.einsum('bhcd,bhdv->bhcv', w_n, S)
        o = jnp.einsum('bhcd,bhdv->bhcv', qd_n, S) + jnp.einsum('bhcs,bhsv->bhcv', qk_n, v_new)
        S = S * jnp.exp(gl_n)[..., None, None] + jnp.einsum('bhcd,bhcv->bhdv', kd_n, v_new)
        return S, o

    xs = (jnp.moveaxis(u, 2, 0), jnp.moveaxis(w, 2, 0), jnp.moveaxis(qk, 2, 0),
          jnp.moveaxis(q_dec, 2, 0), jnp.moveaxis(k_dec, 2, 0), jnp.moveaxis(g_last, 2, 0))
    _, o = lax.scan(step, jnp.zeros((B, H, dk, dv), q.dtype), xs)
    return jnp.moveaxis(o, 0, 2).reshape(B, H, T, dv)


def causal_depthwise_conv(x, w):
    K, C = w.shape
    return lax.conv_general_dilated(x, w[:, None, :], window_strides=(1,), padding=[(K - 1, 0)],
                                    dimension_numbers=('NWC', 'WIO', 'NWC'), feature_group_count=C)


def even_mixer(h, w_in, gate_w2, gate_b, gla_norm_g, w_out):
    B, T, _ = h.shape
    sb_q, sb_k, sb_v, gq, gk, gv, gr, g_lr = _split(h @ w_in, AB_SPLIT)

    def heads(t, n):
        return t.reshape(B, T, n, -1).transpose(0, 2, 1, 3)

    o_sb = stick_breaking_attention(heads(sb_q, SB_HEADS), heads(sb_k, SB_HEADS), heads(sb_v, SB_HEADS))
    o_sb = o_sb.transpose(0, 2, 1, 3).reshape(B, T, SB_WIDTH)

    f32 = jnp.float32
    log_a = jax.nn.log_sigmoid((g_lr @ gate_w2 + gate_b).astype(f32)) / GLA_GATE_NORMALIZER
    o_gla = gla_chunked(heads(gq, GLA_HEADS).astype(f32), heads(gk, GLA_HEADS).astype(f32),
                        heads(gv, GLA_HEADS).astype(f32), heads(log_a, GLA_HEADS))
    o_gla = o_gla.transpose(0, 2, 1, 3)
    o_gla = rms_norm(o_gla, gla_norm_g) * jax.nn.silu(gr.reshape(B, T, GLA_HEADS, GLA_HEAD_V).astype(f32))
    o = jnp.concatenate([o_sb, o_gla.reshape(B, T, GLA_V_WIDTH).astype(h.dtype)], axis=-1)
    return o @ w_out


def odd_mixer(h, w_in, conv_w, a_log, dt_bias, norm_g, w_out):
    B, T, _ = h.shape
    f32 = jnp.float32
    qkv, z, b, a = _split(h @ w_in, GDN_SPLIT)
    qkv = jax.nn.silu(causal_depthwise_conv(qkv, conv_w))
    q, k, v = _split(qkv, (GDN_K_WIDTH, GDN_K_WIDTH, GDN_V_WIDTH))
    rep = GDN_V_HEADS // GDN_K_HEADS
    q = jnp.repeat(l2_normalize(q.reshape(B, T, GDN_K_HEADS, GDN_HEAD_DIM).astype(f32)), rep, axis=2)
    k = jnp.repeat(l2_normalize(k.reshape(B, T, GDN_K_HEADS, GDN_HEAD_DIM).astype(f32)), rep, axis=2)
    v = v.reshape(B, T, GDN_V_HEADS, GDN_HEAD_DIM).astype(f32)
    beta = jax.nn.sigmoid(b.astype(f32))
    g = -jnp.exp(a_log.astype(f32)) * jax.nn.softplus(a.astype(f32) + dt_bias.astype(f32))
    o = gated_delta_rule_chunked(q.transpose(0, 2, 1, 3), k.transpose(0, 2, 1, 3), v.transpose(0, 2, 1, 3),
                                 beta.transpose(0, 2, 1), g.transpose(0, 2, 1))
    o = o.transpose(0, 2, 1, 3)
    o = rms_norm(o, norm_g) * jax.nn.silu(z.reshape(B, T, GDN_V_HEADS, GDN_HEAD_DIM).astype(f32))
    return o.reshape(B, T, GDN_V_WIDTH).astype(h.dtype) @ w_out


def memory_cross_attention(h, mem_k, mem_v, w_q, w_o):
    B, T, _ = h.shape
    q = (h @ w_q).reshape(B, T, MEM_HEADS, MEM_HEAD_DIM)
    s = jnp.einsum('bthd,bmhd->bhtm', q, mem_k).astype(jnp.float32) * MEM_HEAD_DIM ** -0.5
    p = jax.nn.softmax(s, axis=-1).astype(h.dtype)
    o = jnp.einsum('bhtm,bmhd->bthd', p, mem_v).reshape(B, T, D_MODEL)
    return o @ w_o


def swiglu(h, w_gate_up, w_down):
    gate, up = jnp.split(h @ w_gate_up, 2, axis=-1)
    return (jax.nn.silu(gate) * up) @ w_down


def _dense(key, shape, fan_in):
    return jax.random.normal(key, shape, jnp.float32) * fan_in ** -0.5


def _gain(key, shape):
    return 1.0 + 0.02 * jax.random.normal(key, shape, jnp.float32)


def setup_inputs(seed: int = 0) -> dict:
    key = jax.random.key(seed)
    ks = jax.random.split(key, 25)
    ne, no = N_EVEN_LAYERS, N_ODD_LAYERS
    dt = jnp.exp(jax.random.uniform(ks[14], (no, GDN_V_HEADS), jnp.float32, math.log(DT_MIN), math.log(DT_MAX)))
    return {
        'x': jax.random.normal(ks[0], (BATCH, SEQ, D_MODEL), jnp.float32),
        'mem': jax.random.normal(ks[1], (BATCH, MEM_TOKENS, D_MODEL), jnp.float32),
        'mem_norm_g': _gain(ks[2], (D_MODEL,)),
        'mem_w_kv': _dense(ks[3], (D_MODEL, 2 * D_MODEL), D_MODEL),
        'mix_pre_g': _gain(ks[4], (DEPTH, D_MODEL)),
        'mix_post_g': _gain(ks[5], (DEPTH, D_MODEL)),
        'ab_w_in': _dense(ks[6], (ne, D_MODEL, AB_IN_COLS), D_MODEL),
        'gla_gate_w2': _dense(ks[7], (ne, GLA_GATE_RANK, GLA_K_WIDTH), GLA_GATE_RANK),
        'gla_gate_b': 0.1 * jax.random.normal(ks[8], (ne, GLA_K_WIDTH), jnp.float32),
        'gla_norm_g': _gain(ks[9], (ne, GLA_HEAD_V)),
        'ab_w_out': _dense(ks[10], (ne, AB_OUT_WIDTH, D_MODEL), AB_OUT_WIDTH),
        'gdn_w_in': _dense(ks[11], (no, D_MODEL, GDN_IN_COLS), D_MODEL),
        'gdn_conv_w': _dense(ks[12], (no, GDN_CONV_K, GDN_CONV_DIM), GDN_CONV_K),
        'gdn_a_log': jnp.log(jax.random.uniform(ks[13], (no, GDN_V_HEADS), jnp.float32, 1.0, 16.0)),
        'gdn_dt_bias': dt + jnp.log(-jnp.expm1(-dt)),
        'gdn_norm_g': _gain(ks[15], (no, GDN_HEAD_DIM)),
        'gdn_w_out': _dense(ks[16], (no, GDN_V_WIDTH, D_MODEL), GDN_V_WIDTH),
        'xattn_pre_g': _gain(ks[17], (DEPTH, D_MODEL)),
        'xattn_post_g': _gain(ks[18], (DEPTH, D_MODEL)),
        'xattn_w_q': _dense(ks[19], (DEPTH, D_MODEL, D_MODEL), D_MODEL),
        'xattn_w_o': _dense(ks[20], (DEPTH, D_MODEL, D_MODEL), D_MODEL),
        'ffn_pre_g': _gain(ks[21], (DEPTH, D_MODEL)),
        'ffn_post_g': _gain(ks[22], (DEPTH, D_MODEL)),
        'ffn_w_gate_up': _dense(ks[23], (DEPTH, D_MODEL, 2 * FFN_HIDDEN), D_MODEL),
        'ffn_w_down': _dense(ks[24], (DEPTH, FFN_HIDDEN, D_MODEL), FFN_HIDDEN),
    }


def reference(x, mem, mem_norm_g, mem_w_kv, mix_pre_g, mix_post_g, ab_w_in, gla_gate_w2, gla_gate_b,
              gla_norm_g, ab_w_out, gdn_w_in, gdn_conv_w, gdn_a_log, gdn_dt_bias, gdn_norm_g, gdn_w_out,
              xattn_pre_g, xattn_post_g, xattn_w_q, xattn_w_o, ffn_pre_g, ffn_post_g, ffn_w_gate_up, ffn_w_down):
    B, M, _ = mem.shape
    kv = (rms_norm(mem, mem_norm_g) @ mem_w_kv).reshape(B, M, 2, MEM_HEADS, MEM_HEAD_DIM)
    mem_k, mem_v = kv[:, :, 0], kv[:, :, 1]

    h = x
    for layer in range(DEPTH):
        j = layer // 2
        u = rms_norm(h, mix_pre_g[layer])
        if layer % 2 == 0:
            u = even_mixer(u, ab_w_in[j], gla_gate_w2[j], gla_gate_b[j], gla_norm_g[j], ab_w_out[j])
        else:
            u = odd_mixer(u, gdn_w_in[j], gdn_conv_w[j], gdn_a_log[j], gdn_dt_bias[j], gdn_norm_g[j], gdn_w_out[j])
        h = h + rms_norm(u, mix_post_g[layer])
        u = memory_cross_attention(rms_norm(h, xattn_pre_g[layer]), mem_k, mem_v, xattn_w_q[layer], xattn_w_o[layer])
        h = h + rms_norm(u, xattn_post_g[layer])
        u = swiglu(rms_norm(h, ffn_pre_g[layer]), ffn_w_gate_up[layer], ffn_w_down[layer])
        h = h + rms_norm(u, ffn_post_g[layer])
    return h
```

```python
import numpy as np
from contextlib import ExitStack
import concourse.bass as bass
import concourse.mybir as mybir
from concourse.bass_utils import run_bass_kernel_spmd

F32 = mybir.dt.float32
BF16 = mybir.dt.bfloat16
AF = mybir.ActivationFunctionType
ALU = mybir.AluOpType
AX = mybir.AxisListType

D = 2048
T = 2048
KC = D // 128
MEM = 256
DEPTH = 4
FFN_H = 5632
HC = FFN_H // 128
EPS = 1e-6
ACTIVE_CORES = [0, 1, 4, 5]


class Prog:
    ENGS = ("pe", "act", "dve", "pool", "sp")
    NDMA = 8

    def __init__(self, nc, es, name):
        self.nc = nc
        self.name = name
        self.ops = {e: [] for e in self.ENGS}
        self.cnt = {e: 0 for e in self.ENGS}
        self.sem = {e: nc.alloc_semaphore(f"{name}_{e}") for e in ("pe", "act", "dve", "pool")}
        self.dsem = {q: [nc.alloc_semaphore(f"{name}_{q}d{i}") for i in range(self.NDMA)]
                     for q in ("sp", "pool")}
        self.all_sems = list(self.sem.values()) + [x for q in self.dsem.values() for x in q]
        self.dcnt = {"sp": 0, "pool": 0}
        self.known = {e: {} for e in self.ENGS}
        self.res = {}
        self.excl_names = set()

    @staticmethod
    def _base(key):
        while isinstance(key, tuple):
            key = key[0]
        return key

    def _add_wait(self, eng, waits, ev):
        if ev is None:
            return
        sem, val, src = ev
        if src == "pe" and eng == "pe":
            return
        k = self.known[eng]
        if k.get(sem.name, 0) >= val:
            return
        k[sem.name] = val
        waits.append((sem, val))

    def op(self, eng, fn, reads=(), writes=(), dma=False):
        waits = []
        for r in reads:
            st = self.res.get(r)
            if st is not None:
                self._add_wait(eng, waits, st["w"])
                if self._base(r) in self.excl_names:
                    for ev in st["r"]:
                        if ev[2] != eng:
                            self._add_wait(eng, waits, ev)
        for w in writes:
            st = self.res.get(w)
            if st is not None:
                self._add_wait(eng, waits, st["w"])
                for ev in st["r"]:
                    self._add_wait(eng, waits, ev)
        if dma:
            j = self.dcnt[eng]
            self.dcnt[eng] += 1
            sem = self.dsem[eng][j % self.NDMA]
            val = 16 * (j // self.NDMA + 1)
            if j >= self.NDMA:
                self._add_wait(eng, waits, (sem, val - 16, "dma"))
            ev = (sem, val, "dma")
            inc = (sem, 16)
        else:
            self.cnt[eng] += 1
            ev = (self.sem[eng], self.cnt[eng], eng)
            inc = (self.sem[eng], 1)
        for r in reads:
            st = self.res.setdefault(r, {"w": None, "r": []})
            st["r"].append(ev)
        for w in writes:
            self.res[w] = {"w": ev, "r": []}
        self.ops[eng].append((waits, fn, inc))
        return ev

    def dma(self, q, out, in_, reads=(), writes=(), **kw):
        return self.op(q, lambda e: e.dma_start(out=out, in_=in_, **kw), reads, writes, dma=True)

    def emit(self, block):
        def run(eng_name):
            def body(e):
                for waits, fn, inc in self.ops[eng_name]:
                    for sem, val in waits:
                        e.wait_ge(sem, val)
                    ins = fn(e)
                    ins.then_inc(inc[0], inc[1])
                if eng_name in self.dcnt:
                    n = self.dcnt[eng_name]
                    for i in range(min(n, self.NDMA)):
                        tot = (n - 1 - i) // self.NDMA + 1
                        e.wait_ge(self.dsem[eng_name][i], 16 * tot)
            return body
        block.tensor(run("pe"))
        block.scalar(run("act"))
        block.vector(run("dve"))
        block.gpsimd(run("pool"))
        block.sync(run("sp"))


class Phase:
    def __init__(self, nc, name):
        self.nc = nc
        self.name = name

    def __enter__(self):
        self.es = ExitStack()
        self.es.__enter__()
        self.p = Prog(self.nc, self.es, self.name)
        self._n = 0
        return self

    def sb(self, shape, dt, name=None):
        self._n += 1
        return self.es.enter_context(self.nc.sbuf_tensor(f"{self.name}_{name or 't'}{self._n}", list(shape), dt))

    def ps(self, shape, dt=F32, name=None):
        self._n += 1
        self.p.excl_names.add(name)
        esz = 4 if dt == F32 else 2
        t = self.es.enter_context(self.nc.psum_tensor(f"{self.name}_{name or 'p'}{self._n}", [128, 2048 // esz], dt))
        n = int(np.prod(shape[1:]))
        v = t[:shape[0], :n]
        if len(shape) == 3:
            v = v.rearrange("p (a b) -> p a b", a=shape[1])
        return v

    def __exit__(self, *a):
        if a[0] is None:
            with self.nc.Block() as block:
                self.p.emit(block)
            self.nc.all_engine_barrier()
            self.nc.clear_and_free_semaphores(self.p.all_sems)
            self.nc.all_engine_barrier()
        return self.es.__exit__(*a)


class Ring:
    def __init__(self, ph, n, shape, dt, name, psum=False):
        self.bufs = [(ph.ps(shape, dt, name) if psum else ph.sb(shape, dt, name)) for _ in range(n)]
        self.name = name
        self.i = 0

    def next(self):
        k = self.i % len(self.bufs)
        self.i += 1
        return self.bufs[k], (self.name, k)


def load_consts(ph, C):
    p = ph.p
    c = {}
    c["ones_bf"] = ph.sb([128, 128], BF16, "ones")
    p.op("pool", lambda e: e.memset(c["ones_bf"][:], 1.0), writes=["ones_bf"])
    return c


def rstd_from_sumsq(ph, ps_ss, rstd_sb, key_ps, key_out, n, scale):
    p = ph.p
    p.op("act", lambda e: e.activation(out=rstd_sb[:, :n], in_=ps_ss[:, :n], func=AF.Sqrt, bias=ph.eps_ap[:, 0:1], scale=scale),
         reads=[key_ps, "eps"], writes=[key_out])
    p.op("dve", lambda e: e.reciprocal(out=rstd_sb[:, :n], in_=rstd_sb[:, :n]), reads=[key_out], writes=[key_out])


def make_eps(ph):
    ph.eps_ap = ph.sb([128, 1], F32, "eps")
    ph.p.op("pool", lambda e: e.memset(ph.eps_ap[:], EPS), writes=["eps"])


def norm_residual_phase(nc, name, yT, hT, xnT, g_post, g_pre, h_in=None, final_out=None):
    TT = 256
    NT = T // TT
    with Phase(nc, name) as ph:
        p = ph.p
        make_eps(ph)
        ones = ph.sb([128, 128], BF16, "ones")
        p.op("pool", lambda e: e.memset(ones[:], 1.0), writes=["ones"])
        gp = ph.sb([128, KC], F32, "gpost")
        p.dma("sp", gp[:], g_post, writes=["gp"])
        if g_pre is not None:
            gq = ph.sb([128, KC], F32, "gpre")
            p.dma("sp", gq[:], g_pre, writes=["gq"])
        yr = Ring(ph, 3, [128, KC, TT], F32, "y")
        hr = Ring(ph, 3, [128, KC, TT], F32, "h")
        sqr = Ring(ph, 2, [128, KC, TT], BF16, "sq")
        xr = Ring(ph, 2, [128, KC, TT], BF16, "xn")
        rr = Ring(ph, 2, [128, TT], F32, "rstd")
        pss = Ring(ph, 2, [128, TT], F32, "ss", psum=True)
        h_src = hT if h_in is None else h_in
        for tt in range(NT):
            ts = slice(tt * TT, (tt + 1) * TT)
            y, yk = yr.next()
            h, hk = hr.next()
            sq, sqk = sqr.next()
            rs, rk = rr.next()
            ps, pk = pss.next()
            p.dma("sp", y[:], yT[:, :, ts].rearrange("k p t -> p k t"), writes=[yk])
            p.dma("sp", h[:], h_src[:, :, ts].rearrange("k p t -> p k t"), writes=[hk])
            p.op("act", lambda e, sq=sq, y=y: e.activation(out=sq[:], in_=y[:], func=AF.Square), reads=[yk], writes=[sqk])

            def mm(e, ps=ps, sq=sq):
                for k in range(KC):
                    ins = e.matmul(ps[:], ones[:], sq[:, k, :], start=(k == 0), stop=(k == KC - 1))
                return ins
            p.op("pe", mm, reads=[sqk, "ones"], writes=[pk])
            rstd_from_sumsq(ph, ps, rs, pk, rk, TT, 1.0 / D)
            for k in range(KC):
                p.op("dve", lambda e, k=k, y=y, rs=rs: e.scalar_tensor_tensor(
                    out=y[:, k, :], in0=y[:, k, :], scalar=gp[:, k:k + 1], in1=rs[:], op0=ALU.mult, op1=ALU.mult),
                    reads=[yk, rk, "gp"], writes=[yk])
            p.op("pool", lambda e, y=y, h=h: e.tensor_tensor(out=h[:], in0=h[:], in1=y[:], op=ALU.add), reads=[yk, hk], writes=[hk])
            if final_out is not None:
                p.dma("sp", final_out[:, :, ts].rearrange("k p t -> p k t"), h[:], reads=[hk])
            else:
                p.dma("sp", hT[:, :, ts].rearrange("k p t -> p k t"), h[:], reads=[hk])
            if g_pre is not None:
                sq2, sq2k = sqr.next()
                rs2, r2k = rr.next()
                ps2, p2k = pss.next()
                xn, xk = xr.next()
                p.op("act", lambda e, sq2=sq2, h=h: e.activation(out=sq2[:], in_=h[:], func=AF.Square), reads=[hk], writes=[sq2k])

                def mm2(e, ps2=ps2, sq2=sq2):
                    for k in range(KC):
                        ins = e.matmul(ps2[:], ones[:], sq2[:, k, :], start=(k == 0), stop=(k == KC - 1))
                    return ins
                p.op("pe", mm2, reads=[sq2k, "ones"], writes=[p2k])
                rstd_from_sumsq(ph, ps2, rs2, p2k, r2k, TT, 1.0 / D)
                for k in range(KC):
                    eng = "dve"
                    p.op(eng, lambda e, k=k, h=h, rs2=rs2, xn=xn: e.scalar_tensor_tensor(
                        out=xn[:, k, :], in0=h[:, k, :], scalar=gq[:, k:k + 1], in1=rs2[:], op0=ALU.mult, op1=ALU.mult),
                        reads=[hk, r2k, "gq"], writes=[xk])
                p.dma("sp", xnT[:, :, ts].rearrange("k p t -> p k t"), xn[:], reads=[xk])


def prenorm_phase(nc, name, hT, xnT, g_pre):
    TT = 256
    NT = T // TT
    with Phase(nc, name) as ph:
        p = ph.p
        make_eps(ph)
        ones = ph.sb([128, 128], BF16, "ones")
        p.op("pool", lambda e: e.memset(ones[:], 1.0), writes=["ones"])
        gq = ph.sb([128, KC], F32, "gpre")
        p.dma("sp", gq[:], g_pre, writes=["gq"])
        hr = Ring(ph, 2, [128, KC, TT], F32, "h")
        sqr = Ring(ph, 2, [128, KC, TT], BF16, "sq")
        xr = Ring(ph, 2, [128, KC, TT], BF16, "xn")
        rr = Ring(ph, 2, [128, TT], F32, "rstd")
        pss = Ring(ph, 2, [128, TT], F32, "ss", psum=True)
        for tt in range(NT):
            ts = slice(tt * TT, (tt + 1) * TT)
            h, hk = hr.next()
            sq2, sq2k = sqr.next()
            rs2, r2k = rr.next()
            ps2, p2k = pss.next()
            xn, xk = xr.next()
            p.dma("sp", h[:], hT[:, :, ts].rearrange("k p t -> p k t"), writes=[hk])
            p.op("act", lambda e, sq2=sq2, h=h: e.activation(out=sq2[:], in_=h[:], func=AF.Square), reads=[hk], writes=[sq2k])

            def mm2(e, ps2=ps2, sq2=sq2):
                for k in range(KC):
                    ins = e.matmul(ps2[:], ones[:], sq2[:, k, :], start=(k == 0), stop=(k == KC - 1))
                return ins
            p.op("pe", mm2, reads=[sq2k, "ones"], writes=[p2k])
            rstd_from_sumsq(ph, ps2, rs2, p2k, r2k, TT, 1.0 / D)
            for k in range(KC):
                eng = "dve"
                p.op(eng, lambda e, k=k, h=h, rs2=rs2, xn=xn: e.scalar_tensor_tensor(
                    out=xn[:, k, :], in0=h[:, k, :], scalar=gq[:, k:k + 1], in1=rs2[:], op0=ALU.mult, op1=ALU.mult),
                    reads=[hk, r2k, "gq"], writes=[xk])
            p.dma("sp", xnT[:, :, ts].rearrange("k p t -> p k t"), xn[:], reads=[xk])


def ffn_phase(nc, name, xnT, yT, wgu, wd):
    G = 8
    groups = [list(range(s, min(s + G, HC))) for s in range(0, HC, G)]
    with Phase(nc, name) as ph:
        p = ph.p
        xn = ph.sb([128, KC, T], BF16, "xn")
        for k4 in range(4):
            ks = slice(k4 * 4, k4 * 4 + 4)
            p.dma("sp", xn[:, ks, :], xnT[ks].rearrange("k p t -> p k t"), writes=[("xn", k4)])
        xkeys = [("xn", i) for i in range(4)]
        hid = ph.sb([128, G, T], BF16, "hid")
        wdb = ph.sb([128, G, D], BF16, "wd")
        wr = Ring(ph, 3, [128, 2, KC, 128], BF16, "wgu")
        pgr = Ring(ph, 2, [128, 512], F32, "pg", psum=True)
        pur = Ring(ph, 2, [128, 512], F32, "pu", psum=True)
        pyr = Ring(ph, 2, [128, 512], F32, "py", psum=True)
        sgr = Ring(ph, 2, [128, 512], F32, "sg")
        str_ = Ring(ph, 4, [128, 512], F32, "st")
        ev = 0
        for gi, grp in enumerate(groups):
            for hl, hc in enumerate(grp):
                w, wk = wr.next()
                p.dma("pool", w[:, 0], wgu[hc], writes=[(wk, 0)])
                p.dma("pool", w[:, 1], wgu[HC + hc], writes=[(wk, 1)])
                for tt in range(4):
                    ts = slice(tt * 512, (tt + 1) * 512)
                    pg, pgk = pgr.next()
                    pu, puk = pur.next()

                    def mmg(e, w=w, pg=pg, ts=ts):
                        for k in range(KC):
                            ins = e.matmul(pg[:], w[:, 0, k, :], xn[:, k, ts], start=(k == 0), stop=(k == KC - 1))
                        return ins

                    def mmu(e, w=w, pu=pu, ts=ts):
                        for k in range(KC):
                            ins = e.matmul(pu[:], w[:, 1, k, :], xn[:, k, ts], start=(k == 0), stop=(k == KC - 1))
                        return ins
                    p.op("pe", mmg, reads=[(wk, 0)] + xkeys, writes=[pgk])
                    p.op("pe", mmu, reads=[(wk, 1)] + xkeys, writes=[puk])
                    sg, sgk = sgr.next()
                    p.op("act", lambda e, sg=sg, pg=pg: e.activation(out=sg[:], in_=pg[:], func=AF.Silu), reads=[pgk], writes=[sgk])
                    p.op("dve", lambda e, sg=sg, pu=pu, hl=hl, ts=ts: e.tensor_tensor(out=hid[:, hl, ts], in0=sg[:], in1=pu[:], op=ALU.mult),
                         reads=[sgk, puk], writes=[("hid", hl, tt)])
            for hl, hc in enumerate(grp):
                p.dma("pool", wdb[:, hl, :], wd[hc], writes=[("wd", hl)])
            n = len(grp)
            for fo in range(KC):
                for tt in range(4):
                    ts = slice(tt * 512, (tt + 1) * 512)
                    py, pyk = pyr.next()

                    def mmd(e, py=py, fo=fo, ts=ts, n=n):
                        for hl in range(n):
                            ins = e.matmul(py[:], wdb[:, hl, fo * 128:(fo + 1) * 128], hid[:, hl, ts], start=(hl == 0), stop=(hl == n - 1))
                        return ins
                    p.op("pe", mmd, reads=[("wd", hl) for hl in range(n)] + [("hid", hl, tt) for hl in range(n)], writes=[pyk])
                    st, stk = str_.next()
                    if ev % 2 == 0:
                        p.op("act", lambda e, st=st, py=py: e.copy(out=st[:], in_=py[:]), reads=[pyk], writes=[stk])
                    else:
                        p.op("dve", lambda e, st=st, py=py: e.tensor_copy(out=st[:], in_=py[:]), reads=[pyk], writes=[stk])
                    ev += 1
                    kw = {} if gi == 0 else {"accum_op": ALU.add}
                    p.dma("pool", yT[fo, :, ts], st[:], reads=[stk], writes=[("yT", fo, tt)], **kw)


def xattn_phase(nc, name, xnT, yT, oT, wq, wo, kT_d, v_d):
    NH = 4
    sc = float(512 ** -0.5)
    with Phase(nc, name) as ph:
        p = ph.p
        xn = ph.sb([128, KC, T], BF16, "xn")
        for k4 in range(4):
            ks = slice(k4 * 4, k4 * 4 + 4)
            p.dma("sp", xn[:, ks, :], xnT[ks].rearrange("k p t -> p k t"), writes=[("xn", k4)])
        xkeys = [("xn", i) for i in range(4)]
        kT = ph.sb([128, KC, MEM], BF16, "kT")
        vv = ph.sb([128, 2, D], BF16, "v")
        p.dma("sp", kT[:], kT_d, writes=["kT"])
        p.dma("sp", vv[:], v_d, writes=["v"])
        ident = ph.sb([128, 128], BF16, "ident")
        p.dma("pool", ident[:], ph.ident_d, writes=["ident"])
        wr = Ring(ph, 3, [128, KC, 128], BF16, "w")
        qh = ph.sb([128, 4, T], BF16, "qh")
        pT = ph.sb([128, 2, T], BF16, "pT")
        ppr = Ring(ph, 2, [128, 512], F32, "pp", psum=True)
        psr = Ring(ph, 3, [128, MEM], F32, "psc", psum=True)
        ptr = Ring(ph, 3, [128, 2, 128], BF16, "ptr", psum=True)
        er = Ring(ph, 3, [128, MEM], F32, "e")
        pbr = Ring(ph, 3, [128, MEM], BF16, "pb")
        smr = Ring(ph, 6, [128, 4], F32, "sm")
        osr = Ring(ph, 3, [128, 512], BF16, "os")
        ev = 0
        for hd in range(NH):
            for c in range(4):
                fo = 4 * hd + c
                w, wk = wr.next()
                p.dma("pool", w[:], wq[fo], writes=[wk])
                for tt in range(4):
                    ts = slice(tt * 512, (tt + 1) * 512)
                    pp, ppk = ppr.next()

                    def mmq(e, w=w, pp=pp, ts=ts):
                        for k in range(KC):
                            ins = e.matmul(pp[:], w[:, k, :], xn[:, k, ts], start=(k == 0), stop=(k == KC - 1))
                        return ins
                    p.op("pe", mmq, reads=[wk] + xkeys, writes=[ppk])
                    if ev % 2 == 0:
                        p.op("act", lambda e, pp=pp, c=c, ts=ts: e.copy(out=qh[:, c, ts], in_=pp[:]), reads=[ppk], writes=[("qh", c, tt)])
                    else:
                        p.op("dve", lambda e, pp=pp, c=c, ts=ts: e.tensor_copy(out=qh[:, c, ts], in_=pp[:]), reads=[ppk], writes=[("qh", c, tt)])
                    ev += 1
            for t16 in range(16):
                tsl = slice(t16 * 128, (t16 + 1) * 128)
                tt = t16 // 4
                psc, pck = psr.next()

                def mms(e, psc=psc, tsl=tsl, hd=hd):
                    for c in range(4):
                        ins = e.matmul(psc[:], qh[:, c, tsl], kT[:, 4 * hd + c, :], start=(c == 0), stop=(c == 3))
                    return ins
                p.op("pe", mms, reads=[("qh", c, tt) for c in range(4)] + ["kT"], writes=[pck])
                sm, smk = smr.next()
                p.op("dve", lambda e, sm=sm, psc=psc: e.reduce_max(out=sm[:, 0:1], in_=psc[:], axis=AX.X), reads=[pck], writes=[(smk, 0)])
                p.op("dve", lambda e, sm=sm: e.tensor_scalar(out=sm[:, 1:2], in0=sm[:, 0:1], scalar1=-sc, scalar2=None, op0=ALU.mult),
                     reads=[(smk, 0)], writes=[(smk, 1)])
                ee, ek = er.next()
                p.op("act", lambda e, ee=ee, psc=psc, sm=sm: e.activation(out=ee[:], in_=psc[:], func=AF.Exp, bias=sm[:, 1:2], scale=sc, accum_out=sm[:, 2:3]),
                     reads=[pck, (smk, 1)], writes=[ek, (smk, 2)])
                p.op("dve", lambda e, sm=sm: e.reciprocal(out=sm[:, 3:4], in_=sm[:, 2:3]), reads=[(smk, 2)], writes=[(smk, 3)])
                pb, pbk = pbr.next()
                p.op("dve", lambda e, pb=pb, ee=ee, sm=sm: e.tensor_scalar(out=pb[:], in0=ee[:], scalar1=sm[:, 3:4], scalar2=None, op0=ALU.mult),
                     reads=[ek, (smk, 3)], writes=[pbk])
                pt, ptk = ptr.next()

                def mmt(e, pt=pt, pb=pb):
                    for mc in range(2):
                        ins = e.transpose(pt[:, mc, :], pb[:, mc * 128:(mc + 1) * 128], ident[:])
                    return ins
                p.op("pe", mmt, reads=[pbk, "ident"], writes=[ptk])
                p.op("act", lambda e, pt=pt, tsl=tsl: e.copy(out=pT[:, :, tsl], in_=pt[:]), reads=[ptk], writes=[("pT", t16)])
            for c in range(4):
                for tt in range(4):
                    ts = slice(tt * 512, (tt + 1) * 512)
                    pp, ppk = ppr.next()

                    def mmo(e, pp=pp, c=c, ts=ts, hd=hd):
                        for mc in range(2):
                            col = hd * 512 + c * 128
                            ins = e.matmul(pp[:], vv[:, mc, col:col + 128], pT[:, mc, ts], start=(mc == 0), stop=(mc == 1))
                        return ins
                    p.op("pe", mmo, reads=["v"] + [("pT", 4 * tt + i) for i in range(4)], writes=[ppk])
                    os_, osk = osr.next()
                    p.op("dve", lambda e, os_=os_, pp=pp: e.tensor_copy(out=os_[:], in_=pp[:]), reads=[ppk], writes=[osk])
                    p.dma("sp", oT[4 * hd + c, :, ts], os_[:], reads=[osk], writes=[("oT", 4 * hd + c, tt)])
        for k4 in range(4):
            ks = slice(k4 * 4, k4 * 4 + 4)
            p.dma("sp", xn[:, ks, :], oT[ks].rearrange("k p t -> p k t"),
                  reads=[("oT", k, tt) for k in range(k4 * 4, k4 * 4 + 4) for tt in range(4)], writes=[("xn", k4)])
        out_proj(ph, xn, xkeys, KC, wo, yT, wr, ppr)


def out_proj(ph, act_sb, act_keys, kc_n, w_d, yT, wr, ppr):
    p = ph.p
    str_ = Ring(ph, 3, [128, 512], F32, "yst")
    ev = 0
    for fo in range(KC):
        w, wk = wr.next()
        p.dma("pool", w[:, :kc_n, :], w_d[fo], writes=[wk])
        for tt in range(4):
            ts = slice(tt * 512, (tt + 1) * 512)
            pp, ppk = ppr.next()

            def mm(e, w=w, pp=pp, ts=ts):
                for k in range(kc_n):
                    ins = e.matmul(pp[:], w[:, k, :], act_sb[:, k, ts], start=(k == 0), stop=(k == kc_n - 1))
                return ins
            p.op("pe", mm, reads=[wk] + list(act_keys), writes=[ppk])
            st, stk = str_.next()
            if ev % 2 == 0:
                p.op("act", lambda e, st=st, pp=pp: e.copy(out=st[:], in_=pp[:]), reads=[ppk], writes=[stk])
            else:
                p.op("dve", lambda e, st=st, pp=pp: e.tensor_copy(out=st[:], in_=pp[:]), reads=[ppk], writes=[stk])
            ev += 1
            p.dma("sp", yT[fo, :, ts], st[:], reads=[stk], writes=[("yT", fo, tt)])


def memkv_phase(nc, name, memT, g_mem, wkv, kT_d, v_d):
    with Phase(nc, name) as ph:
        p = ph.p
        make_eps(ph)
        ones = ph.sb([128, 128], BF16, "ones")
        p.op("pool", lambda e: e.memset(ones[:], 1.0), writes=["ones"])
        g = ph.sb([128, KC], F32, "g")
        p.dma("sp", g[:], g_mem, writes=["g"])
        m = ph.sb([128, KC, MEM], F32, "m")
        p.dma("sp", m[:], memT.rearrange("k p t -> p k t"), writes=["m"])
        sq = ph.sb([128, KC, MEM], BF16, "sq")
        p.op("act", lambda e: e.activation(out=sq[:], in_=m[:], func=AF.Square), reads=["m"], writes=["sq"])
        ps = ph.ps([128, MEM], F32, "ss")

        def mm(e):
            for k in range(KC):
                ins = e.matmul(ps[:], ones[:], sq[:, k, :], start=(k == 0), stop=(k == KC - 1))
            return ins
        p.op("pe", mm, reads=["sq", "ones"], writes=["ps"])
        rs = ph.sb([128, MEM], F32, "rs")
        rstd_from_sumsq(ph, ps, rs, "ps", "rs", MEM, 1.0 / D)
        mn = ph.sb([128, KC, MEM], BF16, "mn")
        for k in range(KC):
            p.op("dve", lambda e, k=k: e.scalar_tensor_tensor(out=mn[:, k, :], in0=m[:, k, :], scalar=g[:, k:k + 1], in1=rs[:],
                                                              op0=ALU.mult, op1=ALU.mult), reads=["m", "rs", "g"], writes=[("mn", k)])
        mkeys = [("mn", k) for k in range(KC)]
        wr = Ring(ph, 3, [128, KC, 128], BF16, "w")
        ppr = Ring(ph, 2, [128, MEM], F32, "pp", psum=True)
        kT = ph.sb([128, KC, MEM], BF16, "kT")
        vv = ph.sb([128, 2, D], BF16, "v")
        for fo in range(KC):
            w, wk = wr.next()
            p.dma("pool", w[:], wkv[fo], writes=[wk])
            pp, ppk = ppr.next()

            def mmk(e, w=w, pp=pp):
                for k in range(KC):
                    ins = e.matmul(pp[:], w[:, k, :], mn[:, k, :], start=(k == 0), stop=(k == KC - 1))
                return ins
            p.op("pe", mmk, reads=[wk] + mkeys, writes=[ppk])
            p.op("act", lambda e, pp=pp, fo=fo: e.copy(out=kT[:, fo, :], in_=pp[:]), reads=[ppk], writes=[("kT", fo)])
        for fo in range(KC):
            w, wk = wr.next()
            p.dma("pool", w[:], wkv[KC + fo], writes=[wk])
            pp, ppk = ppr.next()

            def mmv(e, w=w, pp=pp):
                for mc in range(2):
                    for k in range(KC):
                        ins = e.matmul(pp[:, mc * 128:(mc + 1) * 128], mn[:, k, mc * 128:(mc + 1) * 128], w[:, k, :],
                                       start=(k == 0), stop=(k == KC - 1))
                return ins
            p.op("pe", mmv, reads=[wk] + mkeys, writes=[ppk])
            p.op("dve", lambda e, pp=pp, fo=fo: e.tensor_copy(out=vv[:, :, fo * 128:(fo + 1) * 128],
                                                             in_=pp[:].rearrange("p (m c) -> p m c", m=2)),
                 reads=[ppk], writes=[("v", fo)])
        p.dma("sp", kT_d, kT[:], reads=[("kT", fo) for fo in range(KC)])
        p.dma("sp", v_d, vv[:], reads=[("v", fo) for fo in range(KC)])


def fm_tiles(W):
    Din, Fo = W.shape
    return np.ascontiguousarray(W.reshape(Din // 128, 128, Fo // 128, 128).transpose(2, 1, 0, 3))


def gain_layout(g):
    return np.ascontiguousarray(g.reshape(-1, 128).T)


def plan_maps(plan):
    ls = sorted(set(plan["layers"]))
    lmap = {l: i for i, l in enumerate(ls)}
    ev = sorted(set(l // 2 for l in ls if l % 2 == 0 and "mix" in plan["subs"]))
    od = sorted(set(l // 2 for l in ls if l % 2 == 1 and "mix" in plan["subs"]))
    return ls, lmap, ev, {j: i for i, j in enumerate(ev)}, od, {j: i for i, j in enumerate(od)}


def build_program(plan, debug=()):
    nc = bass.Bass("TRN2", target_bir_lowering=False)
    I = {}
    ls, lmap, ev, emap, od, omap = plan_maps(plan)
    NL, NE, NO = len(ls), max(len(ev), 1), max(len(od), 1)
    has_x, has_f = "xattn" in plan["subs"], "ffn" in plan["subs"]

    def inp(name, shape, dt=F32):
        I[name] = nc.dram_tensor(name, list(shape), dt, kind="ExternalInput").ap()
        return I[name]

    xT = inp("xT", [KC, 128, T])
    memT = inp("memT", [KC, 128, MEM])
    ident = inp("ident", [128, 128])
    g_mem = inp("g_mem", [128, KC])
    wkv = inp("wkv", [32, 128, KC, 128])
    gains = inp("gains", [NL, 6, 128, KC])
    wq = inp("wq", [NL, KC, 128, KC, 128] if has_x else [1, 1, 128, KC, 128])
    wo = inp("wo", [NL, KC, 128, KC, 128] if has_x else [1, 1, 128, KC, 128])
    wgu = inp("wgu", [NL, 2 * HC, 128, KC, 128] if has_f else [1, 1, 128, KC, 128])
    wd = inp("wd", [NL, HC, 128, D] if has_f else [1, 1, 128, D])
    ab_in = inp("ab_in", [NE, 48, 128, KC, 128] if ev else [1, 1, 128, KC, 128])
    ab_lr = inp("ab_lr", [NE, 128, KC, 16])
    ab_w2b = inp("ab_w2b", [NE, 17, 512])
    ab_gn = inp("ab_gn", [NE, 128, 2])
    ab_out = inp("ab_out", [NE, KC, 128, KC, 128] if ev else [1, 1, 128, KC, 128])
    sbc = inp("sbc", [128, 20, 128])
    gd_in = inp("gd_in", [NO, 96, 128, KC, 128] if od else [1, 1, 128, KC, 128])
    gd_ba = inp("gd_ba", [NO, 128, KC, 64])
    gd_cw = inp("gd_cw", [NO, 64, 128, 4])
    gd_alog = inp("gd_alog", [NO, 128, 4, 32])
    gd_dtb = inp("gd_dtb", [NO, 128, 4, 32])
    gd_gn = inp("gd_gn", [NO, 128, 128])
    gd_out = inp("gd_out", [NO, KC, 128, 32, 128] if od else [1, 1, 128, 32, 128])
    gdc = inp("gdc", [128, 7, 128])
    outT = nc.dram_tensor("outT", [KC, 128, T], F32, kind="ExternalOutput").ap()
    hT = nc.dram_tensor("hT", [KC, 128, T], F32).ap()
    yT = nc.dram_tensor("yT", [KC, 128, T], F32).ap()
    xnT = nc.dram_tensor("xnT", [KC, 128, T], BF16).ap()
    oT = nc.dram_tensor("oT", [32, 128, T], BF16).ap()
    kT_d = nc.dram_tensor("kT_d", [128, KC, MEM], BF16).ap()
    v_d = nc.dram_tensor("v_d", [128, 2, D], BF16).ap()
    S = {
        "sc": nc.dram_tensor("g_sc", [9, 128, 16, 32], F32).ap(),
        "qk": nc.dram_tensor("g_qk", [16, 2, 128, T], BF16).ap(),
        "ktok": nc.dram_tensor("g_ktok", [16, 128, 16, 128], BF16).ap(),
        "vtok": nc.dram_tensor("g_vtok", [32, 128, 16, 128], BF16).ap(),
        "ztok": nc.dram_tensor("g_ztok", [32, 128, 16, 128], BF16).ap(),
    }
    Phase.ident_d = ident

    memkv_phase(nc, "mkv", memT, g_mem, wkv, kT_d, v_d)
    steps = []
    for layer in plan["layers"]:
        for sub in plan["subs"]:
            steps.append((layer, sub))
    first = True
    for i, (layer, sub) in enumerate(steps):
        gi = {"mix": 0, "xattn": 2, "ffn": 4}[sub]
        nm = f"L{layer}{sub[0]}"
        li = lmap[layer]
        j = emap.get(layer // 2, 0) if layer % 2 == 0 else omap.get(layer // 2, 0)
        if first:
            prenorm_phase(nc, nm + "pn", xT, xnT, gains[li, gi])
        if sub == "xattn":
            xattn_phase(nc, nm, xnT, yT, oT, wq[li], wo[li], kT_d, v_d)
        elif sub == "ffn":
            ffn_phase(nc, nm, xnT, yT, wgu[li], wd[li])
        elif layer % 2 == 0:
            parts = plan.get("parts", "sgo")
            if "s" in parts:
                sb_phase(nc, nm + "s", xnT, oT, ab_in[j], sbc)
            if "g" in parts:
                gla_phase(nc, nm + "g", xnT, oT, ab_in[j], ab_lr[j], ab_w2b[j], ab_gn[j], sbc)
            if "o" in parts:
                outproj_phase(nc, nm + "o", oT, yT, ab_out[j], KC)
        else:
            parts = plan.get("parts", "pco")
            if "p" in parts:
                gdn_prep_phase(nc, nm + "p", xnT, gd_in[j], gd_ba[j], gd_cw[j], gd_alog[j], gd_dtb[j], gdc, S)
            if "c" in parts:
                gdn_core_phase(nc, nm + "c", oT, gd_gn[j], gdc, S)
            if "o" in parts:
                outproj_phase(nc, nm + "o", oT, yT, gd_out[j], 32)
        last = (i == len(steps) - 1)
        if last:
            g_next = None
        else:
            nl, ns = steps[i + 1]
            g_next = gains[lmap[nl], {"mix": 0, "xattn": 2, "ffn": 4}[ns]]
        norm_residual_phase(nc, nm + "nr", yT, hT, xnT, gains[li, gi + 1], g_next,
                            h_in=(xT if first else None), final_out=(outT if last else None))
        first = False
    return nc


def prep_inputs(inputs, b):
    f = lambda a: np.ascontiguousarray(np.asarray(a, dtype=np.float32))
    m = {}
    m["xT"] = f(inputs["x"][b].T).reshape(KC, 128, T)
    m["memT"] = f(inputs["mem"][b].T).reshape(KC, 128, MEM)
    return m


def _tile_cols(W):
    Din, n = W.shape
    return np.ascontiguousarray(W.reshape(Din // 128, 128, n).transpose(1, 0, 2))


def prep_shared(inputs, plan):
    f = lambda a: np.asarray(a, dtype=np.float32)
    ls, lmap, ev, emap, od, omap = plan_maps(plan)
    has_x, has_f = "xattn" in plan["subs"], "ffn" in plan["subs"]
    lx = ls if has_x else ls[:1]
    lf = ls if has_f else ls[:1]
    has_e, has_o = bool(ev), bool(od)
    ev = ev or [0]
    od = od or [0]
    s = {}
    s["ident"] = np.eye(128, dtype=np.float32)
    s["g_mem"] = gain_layout(f(inputs["mem_norm_g"]))
    s["wkv"] = fm_tiles(f(inputs["mem_w_kv"]))
    names = ["mix_pre_g", "mix_post_g", "xattn_pre_g", "xattn_post_g", "ffn_pre_g", "ffn_post_g"]
    s["gains"] = np.stack([np.stack([gain_layout(f(inputs[n])[l]) for n in names]) for l in ls])
    s["wq"] = np.stack([fm_tiles(f(inputs["xattn_w_q"][l])) for l in lx])
    s["wo"] = np.stack([fm_tiles(f(inputs["xattn_w_o"][l])) for l in lx])
    s["wgu"] = np.stack([fm_tiles(f(inputs["ffn_w_gate_up"][l])) for l in lf])
    s["wd"] = np.stack([np.ascontiguousarray(f(inputs["ffn_w_down"][l]).reshape(HC, 128, D)) for l in lf])
    abw = inputs["ab_w_in"]
    s["ab_in"] = np.stack([fm_tiles(f(abw[j])[:, :6144]) for j in ev])
    s["ab_lr"] = np.stack([_tile_cols(f(abw[j])[:, 6144:6160]) for j in ev])
    s["ab_w2b"] = np.stack([np.concatenate([f(inputs["gla_gate_w2"])[j], f(inputs["gla_gate_b"])[j][None, :]], 0) for j in ev])
    s["ab_gn"] = np.stack([np.ascontiguousarray(f(inputs["gla_norm_g"])[j].reshape(2, 128).T) for j in ev])
    s["ab_out"] = np.stack([fm_tiles(f(inputs["ab_w_out"][j])) for j in ev])
    jj = np.arange(128)[:, None]
    ss = np.arange(128)[None, :]
    c = np.zeros((128, 20, 128), np.float32)
    c[:, 0] = jj >= ss
    c[:, 1] = jj < ss
    c[:, 2] = jj <= ss
    c[:, 3] = jj > ss
    t512 = np.arange(512)[None, :]
    for v in range(4):
        c[:, 4 + 4 * v:8 + 4 * v, :] = ((jj + 128 * v) < t512).astype(np.float32).reshape(128, 4, 128)
    s["sbc"] = c
    gw = inputs["gdn_w_in"]
    s["gd_in"] = np.stack([fm_tiles(f(gw[j])[:, :12288]) for j in od])
    s["gd_ba"] = np.stack([_tile_cols(f(gw[j])[:, 12288:12352]) for j in od])
    s["gd_cw"] = np.stack([np.ascontiguousarray(f(inputs["gdn_conv_w"])[j].T.reshape(64, 128, 4)) for j in od])
    s["gd_alog"] = np.stack([np.ascontiguousarray(np.broadcast_to(f(inputs["gdn_a_log"])[j], (128, 4, 32))) for j in od])
    s["gd_dtb"] = np.stack([np.ascontiguousarray(np.broadcast_to(f(inputs["gdn_dt_bias"])[j], (128, 4, 32))) for j in od])
    s["gd_gn"] = np.stack([np.ascontiguousarray(np.broadcast_to(f(inputs["gdn_norm_g"])[j], (128, 128))) for j in od])
    s["gd_out"] = np.stack([fm_tiles(f(inputs["gdn_w_out"][j])) for j in od])
    same = (jj // 64) == (ss // 64)
    g = np.zeros((128, 7, 128), np.float32)
    g[:, 0] = np.eye(128)
    g[:, 1] = same & (jj <= ss)
    g[:, 2] = same & (jj > ss)
    g[:, 3] = same & (jj < ss)
    g[:, 4] = (jj < 64) & (ss >= 0)
    g[:, 5] = (jj >= 64) & (ss >= 0)
    g[:, 6] = 1.0
    s["gdc"] = g
    if not has_x:
        s["wq"] = s["wq"][:, :1]
        s["wo"] = s["wo"][:, :1]
    if not has_f:
        s["wgu"] = s["wgu"][:, :1]
        s["wd"] = s["wd"][:, :1]
    if not has_e:
        s["ab_in"] = s["ab_in"][:, :1]
        s["ab_out"] = s["ab_out"][:, :1]
    if not has_o:
        s["gd_in"] = s["gd_in"][:, :1]
        s["gd_out"] = s["gd_out"][:, :1]
    return {k: np.ascontiguousarray(v) for k, v in s.items()}


def run(inputs, plan):
    import time
    t0 = time.time()
    nc = build_program(plan)
    t1 = time.time()
    shared = prep_shared(inputs, plan)
    zero = {k: np.zeros_like(v) for k, v in prep_inputs(inputs, 0).items()}
    zero.update({k: np.zeros_like(v) for k, v in shared.items()})
    in_maps = []
    for c in range(8):
        if c in ACTIVE_CORES:
            m = prep_inputs(inputs, ACTIVE_CORES.index(c))
            m.update(shared)
        else:
            m = zero
        in_maps.append(m)
    t2 = time.time()
    res = run_bass_kernel_spmd(nc, in_maps, core_ids=list(range(8)))
    t3 = time.time()
    print(f"[kernel] build {t1 - t0:.1f}s prep {t2 - t1:.1f}s launch {t3 - t2:.1f}s", flush=True)
    out = np.stack([res.results[c]["outT"].reshape(D, T).T for c in ACTIVE_CORES])
    return np.ascontiguousarray(out.astype(np.float32))


def kernel(**inputs):
    return run(inputs, {"layers": list(range(DEPTH)), "subs": ["mix", "xattn", "ffn"]})


def interleave(gens):
    gens = list(gens)
    while gens:
        for g in list(gens):
            try:
                next(g)
            except StopIteration:
                gens.remove(g)


def load_xn(ph, xnT):
    p = ph.p
    xn = ph.sb([128, KC, T], BF16, "xn")
    for k4 in range(4):
        ks = slice(k4 * 4, k4 * 4 + 4)
        p.dma("sp", xn[:, ks, :], xnT[ks].rearrange("k p t -> p k t"), writes=[("xn", k4)])
    return xn, [("xn", i) for i in range(4)]


def proj_fm(ph, w, wk, xn, xkeys, ppr, evac, m=128):
    p = ph.p
    for tt in range(4):
        ts = slice(tt * 512, (tt + 1) * 512)
        pp, ppk = ppr.next()

        def mm(e, pp=pp, ts=ts):
            for k in range(KC):
                ins = e.matmul(pp[:m, :], w[:, k, :m], xn[:, k, ts], start=(k == 0), stop=(k == KC - 1))
            return ins
        p.op("pe", mm, reads=[wk] + xkeys, writes=[ppk])
        evac(tt, pp, ppk)


def proj_tm(ph, w, wk, xn, xkeys, ppr, evac, n=128):
    p = ph.p
    for t4 in range(4):
        pp, ppk = ppr.next()

        def mm(e, pp=pp, t4=t4):
            for i in range(4):
                t16 = t4 * 4 + i
                for k in range(KC):
                    ins = e.matmul(pp[:, i * n:(i + 1) * n], xn[:, k, t16 * 128:(t16 + 1) * 128], w[:, k, :n],
                                   start=(k == 0), stop=(k == KC - 1))
            return ins
        p.op("pe", mm, reads=[wk] + xkeys, writes=[ppk])
        evac(t4, pp, ppk)


def sb_phase(nc, name, xnT, oT, win, consts):
    scale = float(128 ** -0.5)
    with Phase(nc, name) as ph:
        p = ph.p
        xn, xkeys = load_xn(ph, xnT)
        cst = ph.sb([128, 20, 128], F32, "cst")
        p.dma("sp", cst[:], consts, writes=["cst"])
        GE, LT = cst[:, 0, :], cst[:, 1, :]
        wr = Ring(ph, 3, [128, KC, 128], BF16, "w")
        ppr = Ring(ph, 2, [128, 512], F32, "pp", psum=True)

        def head_stream(h, sid):
            qT = ph.sb([128, T], BF16, "qT")
            kT = ph.sb([128, T], BF16, "kT")
            vt = ph.sb([128, 16, 128], BF16, "vt")
            pz = ph.ps([128, 512], F32, "pz")
            pA = ph.ps([128, 512], F32, "pA")
            po = ph.ps([128, 512], F32, "po")
            er = Ring(ph, 2, [128, 512], F32, f"e{sid}")
            spr = Ring(ph, 2, [128, 512], F32, f"sp{sid}")
            xr = Ring(ph, 2, [128, 512], F32, f"x{sid}")
            wwr = Ring(ph, 2, [128, 512], BF16, f"ww{sid}")
            osr = Ring(ph, 2, [128, 512], BF16, f"os{sid}")
            K = lambda s: (s, sid)
            while h is not None:
                w, wk = wr.next()
                p.dma("pool", w[:], win[h], writes=[wk])
                proj_fm(ph, w, wk, xn, xkeys, ppr, lambda tt, pp, ppk: p.op(
                    "act", lambda e: e.activation(out=qT[:, tt * 512:(tt + 1) * 512], in_=pp[:], func=AF.Copy, scale=scale),
                    reads=[ppk], writes=[K(("qT", tt))]))
                yield
                w, wk = wr.next()
                p.dma("pool", w[:], win[8 + h], writes=[wk])
                proj_fm(ph, w, wk, xn, xkeys, ppr, lambda tt, pp, ppk: p.op(
                    "dve", lambda e: e.tensor_copy(out=kT[:, tt * 512:(tt + 1) * 512], in_=pp[:]),
                    reads=[ppk], writes=[K(("kT", tt))]))
                yield
                w, wk = wr.next()
                p.dma("pool", w[:], win[16 + h], writes=[wk])
                proj_tm(ph, w, wk, xn, xkeys, ppr, lambda t4, pp, ppk: p.op(
                    "act", lambda e: e.copy(out=vt[:, t4 * 4:(t4 + 1) * 4, :], in_=pp[:].rearrange("p (a b) -> p a b", a=4)),
                    reads=[ppk], writes=[K(("vt", t4))]))
                yield
                for qsb in range(4):
                    qs = slice(qsb * 512, (qsb + 1) * 512)
                    nkb = 4 * qsb + 4
                    for i, kb in enumerate(range(nkb - 1, -1, -1)):
                        ks = slice(kb * 128, (kb + 1) * 128)
                        p.op("pe", lambda e, ks=ks, qs=qs: e.matmul(pz[:], kT[:, ks], qT[:, qs], start=True, stop=True),
                             reads=[K(("kT", kb // 4)), K(("qT", qsb))], writes=[K("pz")])
                        ee, ek = er.next()
                        p.op("act", lambda e, ee=ee: e.activation(out=ee[:], in_=pz[:], func=AF.Exp), reads=[K("pz")], writes=[ek])
                        var = kb - 4 * qsb
                        if var >= 0:
                            p.op("pool", lambda e, ee=ee, var=var: e.tensor_tensor(
                                out=ee[:], in0=ee[:], in1=cst[:, 4 + 4 * var:8 + 4 * var, :].rearrange("p a b -> p (a b)"), op=ALU.mult),
                                reads=[ek, "cst"], writes=[ek])
                        yield
                        sp, spk = spr.next()
                        p.op("act", lambda e, ee=ee, sp=sp: e.activation(out=sp[:], in_=ee[:], func=AF.Ln, bias=1.0, scale=1.0),
                             reads=[ek], writes=[spk])
                        p.op("pe", lambda e, sp=sp, i=i: e.matmul(pA[:], GE, sp[:], start=(i == 0), stop=False, skip_group_check=True),
                             reads=[spk, "cst"], writes=[K("pA")])
                        yield
                        xx, xk = xr.next()
                        p.op("act", lambda e, xx=xx: e.activation(out=xx[:], in_=pA[:], func=AF.Exp, scale=-1.0), reads=[K("pA")], writes=[xk])
                        ww, wwk = wwr.next()
                        p.op("dve", lambda e, ww=ww, ee=ee, xx=xx: e.tensor_tensor(out=ww[:], in0=ee[:], in1=xx[:], op=ALU.mult),
                             reads=[ek, xk], writes=[wwk])
                        p.op("pe", lambda e, sp=sp, i=i, nkb=nkb: e.matmul(pA[:], LT, sp[:], start=False, stop=(i == nkb - 1), skip_group_check=True),
                             reads=[spk, "cst", xk], writes=[K("pA")])
                        yield
                        p.op("pe", lambda e, ww=ww, kb=kb, i=i, nkb=nkb: e.matmul(po[:], vt[:, kb, :], ww[:], start=(i == 0), stop=(i == nkb - 1)),
                             reads=[wwk, K(("vt", kb // 4))], writes=[K("pD")])
                    os_, osk = osr.next()
                    p.op("dve", lambda e, os_=os_: e.tensor_copy(out=os_[:], in_=po[:]), reads=[K("pD")], writes=[osk])
                    p.dma("sp", oT[h, :, qs], os_[:], reads=[osk])
                    yield
                h = h + 2 if h + 2 < 8 else None

        interleave([head_stream(0, 0), head_stream(1, 1)])


def gla_phase(nc, name, xnT, oT, win, wlr, w2b, gn_d, consts):
    qscale = float(128 ** -0.5)
    with Phase(nc, name) as ph:
        p = ph.p
        make_eps(ph)
        xn, xkeys = load_xn(ph, xnT)
        cst = ph.sb([128, 4, 128], F32, "cst")
        p.dma("sp", cst[:], consts[:, 0:4, :], writes=["cst"])
        LE, GT = cst[:, 2, :], cst[:, 3, :]
        ones = ph.sb([128, 128], BF16, "ones")
        p.op("pool", lambda e: e.memset(ones[:], 1.0), writes=["ones"])
        gn = ph.sb([128, 2], F32, "gn")
        p.dma("sp", gn[:], gn_d, writes=["gn"])
        w2 = ph.sb([17, 512], F32, "w2")
        p.dma("sp", w2[:], w2b, writes=["w2"])
        wl = ph.sb([128, KC, 16], BF16, "wl")
        p.dma("pool", wl[:], wlr, writes=["wl"])
        wr = Ring(ph, 3, [128, KC, 128], BF16, "w")
        ppr = Ring(ph, 2, [128, 512], F32, "pp", psum=True)
        pss = ph.ps([128, 512], F32, "pss")
        glr = ph.sb([17, T], F32, "glr")
        p.op("pool", lambda e: e.memset(glr[:], 1.0), writes=["glr"])
        proj_fm(ph, wl, "wl", xn, xkeys, ppr, lambda tt, pp, ppk: p.op(
            "act", lambda e: e.copy(out=glr[0:16, tt * 512:(tt + 1) * 512], in_=pp[0:16, :]), reads=[ppk, "glr"], writes=["glr"]), m=16)

        import os
        STOP = int(os.environ.get("GLA_STOP", "9"))
        CUT = int(os.environ.get("GLA_CUT", "9"))

        def head_stream(h, sid):
            K = lambda s: (s, sid)
            if STOP <= 1:
                return
            qT = ph.sb([128, T], BF16, "qT")
            kT = ph.sb([128, T], BF16, "kT")
            kt = ph.sb([128, 16, 128], BF16, "kt")
            vt = ph.sb([128, 16, 256], BF16, "vt")
            sr = ph.sb([128, 2, T], BF16, "sr")
            spt = ph.sb([128, 16, 128], F32, "spt")
            og = ph.sb([128, 2, T], F32, "og")
            sq = ph.sb([128, 2, 512], BF16, "sq")
            S = ph.sb([128, 256], F32, "S")
            Sb = ph.sb([128, 256], BF16, "Sb")
            pa = ph.ps([128, 4, 128], F32, "pa")
            pb = ph.ps([128, 512], F32, "pb")
            t1r = Ring(ph, 2, [128, 512], F32, f"t1{sid}")
            Er = Ring(ph, 2, [128, 2, 128], F32, f"E{sid}")
            qdr = Ring(ph, 2, [128, 128], BF16, f"qd{sid}")
            kir = Ring(ph, 2, [128, 128], BF16, f"ki{sid}")
            dkr = Ring(ph, 2, [128, 128], F32, f"dk{sid}")
            kdr = Ring(ph, 2, [128, 128], BF16, f"kd{sid}")
            STr = Ring(ph, 2, [128, 128], BF16, f"ST{sid}")
            rsr = Ring(ph, 2, [128, 512], F32, f"rs{sid}")
            o1r = Ring(ph, 2, [128, 512], F32, f"o1{sid}")
            obr = Ring(ph, 2, [128, 512], BF16, f"ob{sid}")
            while h is not None:
                def ld(tile):
                    w, wk = wr.next()
                    p.dma("pool", w[:], win[tile], writes=[wk])
                    return w, wk
                w, wk = ld(24 + h)
                proj_fm(ph, w, wk, xn, xkeys, ppr, lambda tt, pp, ppk: p.op(
                    "act", lambda e: e.copy(out=qT[:, tt * 512:(tt + 1) * 512], in_=pp[:]), reads=[ppk], writes=[K(("qT", tt))]))
                yield
                w, wk = ld(28 + h)
                proj_fm(ph, w, wk, xn, xkeys, ppr, lambda tt, pp, ppk: p.op(
                    "dve", lambda e: e.tensor_copy(out=kT[:, tt * 512:(tt + 1) * 512], in_=pp[:]), reads=[ppk], writes=[K(("kT", tt))]))
                proj_tm(ph, w, wk, xn, xkeys, ppr, lambda t4, pp, ppk: p.op(
                    "act", lambda e: e.copy(out=kt[:, t4 * 4:(t4 + 1) * 4, :], in_=pp[:].rearrange("p (a b) -> p a b", a=4)),
                    reads=[ppk], writes=[K(("kt", t4))]))
                yield
                for vc in range(2):
                    w, wk = ld(32 + 2 * h + vc)
                    proj_tm(ph, w, wk, xn, xkeys, ppr, lambda t4, pp, ppk, vc=vc: p.op(
                        "dve", lambda e: e.tensor_copy(out=vt[:, t4 * 4:(t4 + 1) * 4, vc * 128:(vc + 1) * 128],
                                                       in_=pp[:].rearrange("p (a b) -> p a b", a=4)),
                        reads=[ppk, K(("vt", t4))], writes=[K(("vt", t4))]))
                    yield
                for vc in range(2):
                    w, wk = ld(40 + 2 * h + vc)
                    proj_fm(ph, w, wk, xn, xkeys, ppr, lambda tt, pp, ppk, vc=vc: p.op(
                        "act", lambda e: e.activation(out=sr[:, vc, tt * 512:(tt + 1) * 512], in_=pp[:], func=AF.Silu),
                        reads=[ppk], writes=[K(("sr", vc, tt))]))
                    yield
                if STOP <= 2:
                    return
                for t4 in range(4):
                    pp, ppk = ppr.next()

                    def mmx(e, pp=pp, t4=t4, h=h):
                        for i in range(4):
                            t16 = t4 * 4 + i
                            ins = e.matmul(pp[:, i * 128:(i + 1) * 128], glr[:, t16 * 128:(t16 + 1) * 128], w2[:, h * 128:(h + 1) * 128],
                                           start=True, stop=True)
                        return ins
                    p.op("pe", mmx, reads=["glr", "w2"], writes=[ppk])
                    t1, t1k = t1r.next()
                    p.op("act", lambda e, t1=t1, pp=pp: e.activation(out=t1[:], in_=pp[:], func=AF.Exp, scale=-1.0), reads=[ppk], writes=[t1k])
                    p.op("act", lambda e, t1=t1, t4=t4: e.activation(out=spt[:, t4 * 4:(t4 + 1) * 4, :].rearrange("p a b -> p (a b)"), in_=t1[:],
                                                                   func=AF.Ln, bias=1.0, scale=1.0), reads=[t1k], writes=[K(("spt", t4))])
                    yield
                if STOP <= 3:
                    return
                p.op("pool", lambda e: e.memset(S[:], 0.0), reads=[K("S")], writes=[K("S")])
                p.op("pool", lambda e: e.memset(Sb[:], 0.0), reads=[K("Sb")], writes=[K("Sb")])
                for n in range(16):
                    cs = slice(n * 128, (n + 1) * 128)
                    tt = n // 4
                    p.op("pe", lambda e, n=n: e.matmul(pa[:, 0, :], spt[:, n, :], LE, start=True, stop=True),
                         reads=[K(("spt", n // 4)), "cst"], writes=[K("pa")])
                    p.op("pe", lambda e, n=n: e.matmul(pa[:, 1, :], GT, spt[:, n, :], start=True, stop=True),
                         reads=[K(("spt", n // 4)), "cst"], writes=[K("pa")])
                    E, Ek = Er.next()
                    p.op("act", lambda e, E=E: e.activation(out=E[:, 0, :], in_=pa[:, 0, :], func=AF.Exp, scale=-1.0 / 16), reads=[K("pa")], writes=[(Ek, 0)])
                    p.op("act", lambda e, E=E: e.activation(out=E[:, 1, :], in_=pa[:, 0, :], func=AF.Exp, scale=1.0 / 16), reads=[K("pa")], writes=[(Ek, 1)])
                    dk_, dkk = dkr.next()
                    p.op("act", lambda e, dk_=dk_: e.activation(out=dk_[:], in_=pa[:, 1, :], func=AF.Exp, scale=-1.0 / 16), reads=[K("pa")], writes=[dkk])
                    yield
                    if CUT <= 1:
                        return
                    qd, qdk = qdr.next()
                    ki, kik = kir.next()
                    kd, kdk = kdr.next()
                    p.op("dve", lambda e, qd=qd, E=E, cs=cs: e.scalar_tensor_tensor(out=qd[:], in0=qT[:, cs], scalar=qscale, in1=E[:, 0, :],
                                                                                     op0=ALU.mult, op1=ALU.mult),
                         reads=[K(("qT", tt)), (Ek, 0)], writes=[qdk])
                    p.op("dve", lambda e, ki=ki, E=E, cs=cs: e.tensor_tensor(out=ki[:], in0=kT[:, cs], in1=E[:, 1, :], op=ALU.mult),
                         reads=[K(("kT", tt)), (Ek, 1)], writes=[kik])
                    p.op("pool", lambda e, kd=kd, dk_=dk_, n=n: e.tensor_tensor(out=kd[:], in0=kt[:, n, :], in1=dk_[:], op=ALU.mult),
                         reads=[K(("kt", n // 4)), dkk], writes=[kdk])
                    p.op("pe", lambda e, ki=ki, qd=qd: e.matmul(pa[:, 2, :], ki[:], qd[:], start=True, stop=True), reads=[kik, qdk], writes=[K("pa")])
                    yield
                    if CUT <= 2:
                        return
                    ST, STk = STr.next()
                    p.op("dve", lambda e, ST=ST: e.tensor_tensor(out=ST[:], in0=pa[:, 2, :], in1=LE, op=ALU.mult), reads=[K("pa"), "cst"], writes=[STk])

                    def mmo(e, ST=ST, qd=qd, n=n):
                        for vc in range(2):
                            e.matmul(pb[:, vc * 128:(vc + 1) * 128], vt[:, n, vc * 128:(vc + 1) * 128], ST[:], start=True, stop=False)
                            ins = e.matmul(pb[:, vc * 128:(vc + 1) * 128], Sb[:, vc * 128:(vc + 1) * 128], qd[:], start=False, stop=True)
                        return ins
                    p.op("pe", mmo, reads=[STk, qdk, K(("vt", n // 4)), K("Sb")], writes=[K("pb")])
                    p.op("pe", lambda e, kd=kd, n=n: e.matmul(pb[:, 256:512], kd[:], vt[:, n, :], start=True, stop=True),
                         reads=[kdk, K(("vt", n // 4))], writes=[K("pb")])
                    yield
                    if CUT <= 3:
                        return
                    p.op("act", lambda e, cs=cs: e.copy(out=og[:, :, cs], in_=pb[:, 0:256].rearrange("p (a b) -> p a b", a=2)),
                         reads=[K("pb")], writes=[K(("og", n))])
                    p.op("dve", lambda e, E=E: e.scalar_tensor_tensor(out=S[:], in0=S[:], scalar=E[:, 0, 127:128], in1=pb[:, 256:512],
                                                                      op0=ALU.mult, op1=ALU.add),
                         reads=[K("S"), (Ek, 0), K("pb")], writes=[K("S")])
                    p.op("act", lambda e: e.copy(out=Sb[:], in_=S[:]), reads=[K("S")], writes=[K("Sb")])
                    yield
                    if CUT <= 4:
                        return
                    if CUT == 5 and n == 1:
                        return
                if STOP <= 4:
                    return
                for tt in range(4):
                    ts = slice(tt * 512, (tt + 1) * 512)
                    okeys = [K(("og", n)) for n in range(tt * 4, tt * 4 + 4)]
                    p.op("act", lambda e, ts=ts: e.activation(out=sq[:], in_=og[:, :, ts], func=AF.Square), reads=okeys, writes=[K("sq")])

                    def mms(e):
                        e.matmul(pss[:], ones[:], sq[:, 0, :], start=True, stop=False)
                        return e.matmul(pss[:], ones[:], sq[:, 1, :], start=False, stop=True)
                    p.op("pe", mms, reads=[K("sq"), "ones"], writes=["pss"])
                    rs, rsk = rsr.next()
                    rstd_from_sumsq(ph, pss, rs, "pss", rsk, 512, 1.0 / 256)
                    for vc in range(2):
                        o1, o1k = o1r.next()
                        ob, obk = obr.next()
                        p.op("dve", lambda e, o1=o1, rs=rs, vc=vc, ts=ts: e.scalar_tensor_tensor(
                            out=o1[:], in0=og[:, vc, ts], scalar=gn[:, vc:vc + 1], in1=rs[:], op0=ALU.mult, op1=ALU.mult),
                            reads=okeys + [rsk, "gn"], writes=[o1k])
                        p.op("pool", lambda e, o1=o1, ob=ob, vc=vc, ts=ts: e.tensor_tensor(out=ob[:], in0=o1[:], in1=sr[:, vc, ts], op=ALU.mult),
                             reads=[o1k, K(("sr", vc, tt))], writes=[obk])
                        p.dma("sp", oT[8 + 2 * h + vc, :, ts], ob[:], reads=[obk])
                    yield
                h = h + 1 if h + 1 < 4 else None

        interleave([head_stream(0, 0)])


def outproj_phase(nc, name, oT, yT, w_d, kc_n):
    with Phase(nc, name) as ph:
        p = ph.p
        o = ph.sb([128, kc_n, T], BF16, "o")
        keys = []
        for k4 in range(kc_n // 4):
            ks = slice(k4 * 4, k4 * 4 + 4)
            p.dma("sp", o[:, ks, :], oT[ks].rearrange("k p t -> p k t"), writes=[("o", k4)])
            keys.append(("o", k4))
        wr = Ring(ph, 2, [128, kc_n, 128], BF16, "w")
        ppr = Ring(ph, 2, [128, 512], F32, "pp", psum=True)
        out_proj(ph, o, keys, kc_n, w_d, yT, wr, ppr)


def gdn_prep_phase(nc, name, xnT, win, wba, cw_d, alog_d, dtb_d, gc_d, S):
    qscale = float(128 ** -0.5)
    with Phase(nc, name) as ph:
        p = ph.p
        make_eps(ph)
        xn, xkeys = load_xn(ph, xnT)
        cst = ph.sb([128, 7, 128], F32, "cst")
        p.dma("sp", cst[:], gc_d, writes=["cst"])
        ident, BDLE, BDGT, CA, CB = cst[:, 0, :], cst[:, 1, :], cst[:, 2, :], cst[:, 4, :], cst[:, 5, :]
        ones = ph.sb([128, 128], BF16, "ones")
        p.op("pool", lambda e: e.memset(ones[:], 1.0), writes=["ones"])
        wr = Ring(ph, 3, [128, KC, 128], BF16, "w")
        ppr = Ring(ph, 2, [128, 512], F32, "pp", psum=True)
        ptr_ = Ring(ph, 2, [128, 512], F32, "pt", psum=True)
        pss = ph.ps([128, 512], F32, "pss")
        pgg = ph.ps([128, 4, 32], F32, "pgg")
        wb = ph.sb([128, KC, 64], BF16, "wb")
        p.dma("pool", wb[:], wba, writes=["wb"])
        alog = ph.sb([128, 4, 32], F32, "alog")
        dtb = ph.sb([128, 4, 32], F32, "dtb")
        p.dma("sp", alog[:], alog_d, writes=["alog"])
        p.dma("sp", dtb[:], dtb_d, writes=["dtb"])
        nea = ph.sb([128, 4, 32], F32, "nea")
        p.op("act", lambda e: e.activation(out=nea[:], in_=alog[:], func=AF.Exp), reads=["alog"], writes=["nea"])
        p.op("dve", lambda e: e.tensor_scalar(out=nea[:], in0=nea[:], scalar1=-1.0, scalar2=None, op0=ALU.mult), reads=["nea"], writes=["nea"])
        beta = ph.sb([128, 16, 32], F32, "beta")
        nbeta = ph.sb([128, 16, 32], F32, "nbeta")
        gt = ph.sb([128, 16, 32], F32, "gt")
        e4 = ph.sb([128, 4, 16, 32], F32, "e4")
        tmr = Ring(ph, 2, [128, 4, 32], F32, "tm")

        def ba_evac(t4, pp, ppk):
            v = pp[:, 0:256].rearrange("p (a b) -> p a b", a=4)
            t4s = slice(t4 * 4, (t4 + 1) * 4)
            t1, t1k = tmr.next()
            p.op("act", lambda e: e.activation(out=t1[:], in_=v[:, :, 0:32], func=AF.Exp, scale=-1.0), reads=[ppk], writes=[t1k])
            p.op("dve", lambda e: e.tensor_scalar(out=t1[:], in0=t1[:], scalar1=1.0, scalar2=None, op0=ALU.add), reads=[t1k], writes=[t1k])
            p.op("dve", lambda e: e.reciprocal(out=beta[:, t4s, :], in_=t1[:]), reads=[t1k], writes=[("beta", t4)])
            p.op("pool", lambda e: e.tensor_scalar(out=nbeta[:, t4s, :], in0=beta[:, t4s, :], scalar1=-1.0, scalar2=None, op0=ALU.mult),
                 reads=[("beta", t4)], writes=[("nbeta", t4)])
            t2, t2k = tmr.next()
            p.op("dve", lambda e: e.tensor_tensor(out=t2[:], in0=v[:, :, 32:64], in1=dtb[:], op=ALU.add), reads=[ppk, "dtb"], writes=[t2k])
            p.op("act", lambda e: e.activation(out=t2[:], in_=t2[:], func=AF.Exp), reads=[t2k], writes=[t2k])
            p.op("act", lambda e: e.activation(out=t2[:], in_=t2[:], func=AF.Ln, bias=1.0, scale=1.0), reads=[t2k], writes=[t2k])
            p.op("dve", lambda e: e.tensor_tensor(out=gt[:, t4s, :], in0=t2[:], in1=nea[:], op=ALU.mult), reads=[t2k, "nea"], writes=[("gt", t4)])
        proj_tm(ph, wb, "wb", xn, xkeys, ppr, ba_evac, n=64)
        for n in range(16):
            def mmg(e, n=n):
                e.matmul(pgg[:, 0, :], BDLE, gt[:, n, :], start=True, stop=True)
                e.matmul(pgg[:, 1, :], BDGT, gt[:, n, :], start=True, stop=True)
                e.matmul(pgg[:, 2, :], CA, gt[:, n, :], start=True, stop=True)
                return e.matmul(pgg[:, 3, :], CB, gt[:, n, :], start=True, stop=True)
            p.op("pe", mmg, reads=[("gt", n // 4), "cst"], writes=["pgg"])
            p.op("act", lambda e, n=n: e.activation(out=e4[:, :, n, :], in_=pgg[:], func=AF.Exp), reads=["pgg"], writes=[("e4", n)])
        p.dma("sp", S["sc"][0], beta[:], reads=[("beta", i) for i in range(4)])
        p.dma("sp", S["sc"][1], nbeta[:], reads=[("nbeta", i) for i in range(4)])
        for j in range(4):
            p.dma("sp", S["sc"][2 + j], e4[:, j], reads=[("e4", n) for n in range(16)])
        p.dma("sp", S["sc"][6], gt[:], reads=[("gt", i) for i in range(4)])
        ekm = ph.sb([128, 2, 16, 32], F32, "ekm")
        for x, cm in enumerate((CA, CB)):
            p.op("dve", lambda e, x=x, cm=cm: e.tensor_scalar(out=ekm[:, x], in0=e4[:, 1], scalar1=cm[:, 0:1], scalar2=None, op0=ALU.mult),
                 reads=[("e4", n) for n in range(16)] + ["cst"], writes=[("ekm", x)])
            p.dma("sp", S["sc"][7 + x], ekm[:, x], reads=[("ekm", x)])

        NS = 2
        xc = [ph.sb([128, 3 + T], F32, "xc") for _ in range(NS)]
        acc = [ph.sb([128, T], F32, "acc") for _ in range(NS)]
        cs = [ph.sb([128, T], F32, "cs") for _ in range(NS)]
        sq = [ph.sb([128, T], BF16, "sq") for _ in range(NS)]
        rs = [ph.sb([128, T], F32, "rs") for _ in range(NS)]
        csb = [ph.sb([128, T], BF16, "csb") for _ in range(NS)]
        tokb = [ph.sb([128, 16, 128], BF16, "tokb") for _ in range(NS)]
        for i in range(NS):
            p.op("pool", lambda e, i=i: e.memset(xc[i][:, 0:3], 0.0), writes=[("xc0", i)])
        cwr = Ring(ph, 2, [128, 4], F32, "cw")
        slot = [0]

        def conv_tile(tile):
            s_ = slot[0] % NS
            slot[0] += 1
            w, wk = wr.next()
            p.dma("pool", w[:], win[tile], writes=[wk])
            cw, cwk = cwr.next()
            p.dma("sp", cw[:], cw_d[tile], writes=[cwk])
            proj_fm(ph, w, wk, xn, xkeys, ppr, lambda tt, pp, ppk: p.op(
                "act", lambda e: e.copy(out=xc[s_][:, 3 + tt * 512:3 + (tt + 1) * 512], in_=pp[:]), reads=[ppk], writes=[("xc", s_, tt)]))
            xk = [("xc", s_, tt) for tt in range(4)] + [("xc0", s_)]
            p.op("dve", lambda e: e.tensor_scalar(out=acc[s_][:], in0=xc[s_][:, 0:T], scalar1=cw[:, 0:1], scalar2=None, op0=ALU.mult),
                 reads=xk + [cwk], writes=[("acc", s_)])
            for i in range(1, 4):
                p.op("dve", lambda e, i=i: e.scalar_tensor_tensor(out=acc[s_][:], in0=xc[s_][:, i:i + T], scalar=cw[:, i:i + 1], in1=acc[s_][:],
                                                                 op0=ALU.mult, op1=ALU.add), reads=xk + [cwk, ("acc", s_)], writes=[("acc", s_)])
            p.op("act", lambda e: e.activation(out=cs[s_][:], in_=acc[s_][:], func=AF.Silu), reads=[("acc", s_)], writes=[("cs", s_)])
            return s_

        def l2n(s_, scl):
            p.op("act", lambda e: e.activation(out=sq[s_][:], in_=cs[s_][:], func=AF.Square), reads=[("cs", s_)], writes=[("sq", s_)])
            for tt in range(4):
                ts = slice(tt * 512, (tt + 1) * 512)
                p.op("pe", lambda e, ts=ts: e.matmul(pss[:], ones[:], sq[s_][:, ts], start=True, stop=True), reads=[("sq", s_), "ones"], writes=["pss"])
                p.op("act", lambda e, ts=ts: e.activation(out=rs[s_][:, ts], in_=pss[:], func=AF.Sqrt, bias=ph.eps_ap[:, 0:1], scale=1.0),
                     reads=["pss", "eps"], writes=[("rs", s_, tt)])
            rk = [("rs", s_, tt) for tt in range(4)]
            p.op("dve", lambda e: e.reciprocal(out=rs[s_][:], in_=rs[s_][:]), reads=rk, writes=rk)
            p.op("dve", lambda e: e.scalar_tensor_tensor(out=cs[s_][:], in0=cs[s_][:], scalar=scl, in1=rs[s_][:], op0=ALU.mult, op1=ALU.mult),
                 reads=[("cs", s_)] + rk, writes=[("cs", s_)])

        def to_tok(s_):
            dst = tokb[s_]
            for t4 in range(4):
                pt, ptk = ptr_.next()

                def mmt(e, pt=pt, t4=t4):
                    for i in range(4):
                        t16 = t4 * 4 + i
                        ins = e.transpose(pt[:, i * 128:(i + 1) * 128], cs[s_][:, t16 * 128:(t16 + 1) * 128], ident)
                    return ins
                p.op("pe", mmt, reads=[("cs", s_), "cst"], writes=[ptk])
                if t4 % 2 == 0:
                    p.op("act", lambda e, pt=pt, t4=t4: e.copy(out=dst[:, t4 * 4:(t4 + 1) * 4, :], in_=pt[:].rearrange("p (a b) -> p a b", a=4)),
                         reads=[ptk], writes=[("tokb", s_, t4)])
                else:
                    p.op("dve", lambda e, pt=pt, t4=t4: e.tensor_copy(out=dst[:, t4 * 4:(t4 + 1) * 4, :], in_=pt[:].rearrange("p (a b) -> p a b", a=4)),
                         reads=[ptk], writes=[("tokb", s_, t4)])
            return [("tokb", s_, i) for i in range(4)]

        def store_fm(s_, dst):
            p.op("act", lambda e: e.copy(out=csb[s_][:], in_=cs[s_][:]), reads=[("cs", s_)], writes=[("csb", s_)])
            p.dma("sp", dst, csb[s_][:], reads=[("csb", s_)])

        jobs = []
        for hk in range(16):
            jobs.append(("q", hk, hk))
            jobs.append(("k", hk, 16 + hk))
        for hv in range(32):
            jobs.append(("v", hv, 32 + hv))

        def finish(job, s_):
            kind, idx, _ = job
            if kind == "q":
                l2n(s_, qscale)
                store_fm(s_, S["qk"][idx, 0])
            elif kind == "k":
                l2n(s_, 1.0)
                store_fm(s_, S["qk"][idx, 1])
                keys = to_tok(s_)
                p.dma("sp", S["ktok"][idx], tokb[s_][:], reads=keys)
            else:
                keys = to_tok(s_)
                p.dma("sp", S["vtok"][idx], tokb[s_][:], reads=keys)
        prev = None
        for job in jobs:
            s_ = conv_tile(job[2])
            if prev is not None:
                finish(*prev)
            prev = (job, s_)
        finish(*prev)
        ztr = Ring(ph, 2, [128, 16, 128], BF16, "zt")
        for hv in range(32):
            w, wk = wr.next()
            p.dma("pool", w[:], win[64 + hv], writes=[wk])
            zt, ztk = ztr.next()
            proj_tm(ph, w, wk, xn, xkeys, ppr, lambda t4, pp, ppk, zt=zt, ztk=ztk: p.op(
                "act", lambda e: e.activation(out=zt[:, t4 * 4:(t4 + 1) * 4, :], in_=pp[:].rearrange("p (a b) -> p a b", a=4), func=AF.Silu),
                reads=[ppk], writes=[(ztk, t4)]))
            p.dma("sp", S["ztok"][hv], zt[:], reads=[(ztk, i) for i in range(4)])


def gdn_core_phase(nc, name, oT, gnrow_d, gc_d, S, nstreams=4):
    with Phase(nc, name) as ph:
        p = ph.p
        make_eps(ph)
        cst = ph.sb([128, 7, 128], F32, "cst")
        p.dma("sp", cst[:], gc_d, writes=["cst"])
        ident, BDLE, BDGT, BDLT, CA, CB, ONES = [cst[:, i, :] for i in range(7)]
        identb = ph.sb([128, 128], BF16, "identb")
        p.op("act", lambda e: e.copy(out=identb[:], in_=ident), reads=["cst"], writes=["identb"])
        onesb = ph.sb([128, 128], BF16, "onesb")
        p.op("act", lambda e: e.copy(out=onesb[:], in_=ONES), reads=["cst"], writes=["onesb"])
        gnrow = ph.sb([128, 128], F32, "gnrow")
        p.dma("sp", gnrow[:], gnrow_d, writes=["gnrow"])
        sc = ph.sb([128, 8, 16, 32], F32, "sc")
        for j, src in enumerate((0, 1, 2, 3, 4, 5, 7, 8)):
            p.dma("sp", sc[:, j], S["sc"][src], writes=[("sc", j)])
        sck = [("sc", j) for j in range(8)]
        BETA, NBETA, EGC, EKD, ELA, ELB, EKDA, EKDB = range(8)

        def stream(sid):
            K = lambda s: (s, sid)
            qT = ph.sb([128, T], BF16, "qT")
            kT = ph.sb([128, T], BF16, "kT")
            ktok = ph.sb([128, 16, 128], BF16, "ktok")
            vtok = ph.sb([128, 16, 128], BF16, "vtok")
            ztok = ph.sb([128, 16, 128], BF16, "ztok")
            oTs = ph.sb([128, T], BF16, "oTs")
            Sst = ph.sb([128, 128], F32, "S")
            Sb = ph.sb([128, 128], BF16, "Sb")
            pA = ph.ps([128, 4, 128], F32, "pA")
            pB = pA
            pC = ph.ps([128, 512], F32, "pC")
            pD = pC.rearrange("p (a b) -> p a b", a=4)
            mk = lambda nm, shp, dt: Ring(ph, 2, shp, dt, f"{nm}{sid}")
            GMr, edr, dsr, dir_ = mk("GM", [128, 128], F32), mk("ed", [128, 128], F32), mk("ds", [128, 128], F32), mk("di", [128, 128], F32)
            Yr, YTr, Mr = mk("Y", [128, 128], F32), mk("YT", [128, 128], F32), mk("M", [128, 128], F32)
            QKr, Mbr, Rr = mk("QK", [128, 128], BF16), mk("Mb", [128, 128], BF16), mk("kg", [128, 128], BF16)
            dgr, bcr = mk("dg", [128, 256], BF16), mk("bc", [128, 128], F32)
            ur, wTr, qpr, kdr = mk("u", [128, 128], F32), mk("wT", [128, 128], BF16), mk("qp", [128, 128], BF16), mk("kd", [128, 2, 128], BF16)
            vnr, osr, smr, ogr, otr = mk("vn", [128, 128], BF16), mk("o", [128, 128], F32), mk("sm", [128, 4], F32), mk("og", [128, 128], BF16), mk("ot", [128, 128], F32)
            for hv in range(sid, 32, nstreams):
                hk = hv // 2
                p.dma("sp", qT[:], S["qk"][hk, 0], writes=[K("qT")])
                p.dma("sp", kT[:], S["qk"][hk, 1], writes=[K("kT")])
                p.dma("sp", ktok[:], S["ktok"][hk], writes=[K("ktok")])
                p.dma("sp", vtok[:], S["vtok"][hv], writes=[K("vtok")])
                p.dma("sp", ztok[:], S["ztok"][hv], writes=[K("ztok")])
                p.op("pool", lambda e: e.memset(Sst[:], 0.0), reads=[K("S")], writes=[K("S")])
                p.op("pool", lambda e: e.memset(Sb[:], 0.0), reads=[K("Sb")], writes=[K("Sb")])
                for n in range(16):
                    cs = slice(n * 128, (n + 1) * 128)
                    col = lambda j, n=n, hv=hv: sc[:, j, n, hv:hv + 1]
                    GM, GMk = GMr.next()
                    p.op("dve", lambda e, GM=GM, n=n, hv=hv: e.tensor_scalar(out=GM[:], in0=BDGT, scalar1=S_g[:, n, hv:hv + 1], scalar2=None, op0=ALU.mult),
                         reads=["cst", "gtl"], writes=[GMk])
                    p.op("pe", lambda e, GM=GM: e.matmul(pA[:, 2, :], GM[:], BDLE, start=True, stop=True), reads=[GMk, "cst"], writes=[K("pA")])
                    p.op("pe", lambda e, cs=cs: e.matmul(pA[:, 0, :], kT[:, cs], kT[:, cs], start=True, stop=True), reads=[K("kT")], writes=[K("pA")])
                    p.op("pe", lambda e, cs=cs: e.matmul(pA[:, 1, :], kT[:, cs], qT[:, cs], start=True, stop=True), reads=[K("kT"), K("qT")], writes=[K("pA")])
                    ed, edk = edr.next()
                    p.op("act", lambda e, ed=ed: e.activation(out=ed[:], in_=pA[:, 2, :], func=AF.Exp), reads=[K("pA")], writes=[edk])
                    ds, dsk = dsr.next()
                    di, dik = dir_.next()
                    p.op("dve", lambda e, ds=ds, ed=ed: e.tensor_tensor(out=ds[:], in0=ed[:], in1=BDLT, op=ALU.mult), reads=[edk, "cst"], writes=[dsk])
                    p.op("pool", lambda e, di=di, ed=ed: e.tensor_tensor(out=di[:], in0=ed[:], in1=BDLE, op=ALU.mult), reads=[edk, "cst"], writes=[dik])
                    yield
                    Y, Yk = Yr.next()
                    p.op("dve", lambda e, Y=Y, ds=ds, col=col: e.scalar_tensor_tensor(out=Y[:], in0=pA[:, 0, :], scalar=col(NBETA), in1=ds[:],
                                                                                     op0=ALU.mult, op1=ALU.mult),
                         reads=[K("pA"), dsk] + sck, writes=[Yk])
                    QK, QKk = QKr.next()
                    p.op("dve", lambda e, QK=QK, di=di: e.tensor_tensor(out=QK[:], in0=pA[:, 1, :], in1=di[:], op=ALU.mult),
                         reads=[K("pA"), dik], writes=[QKk])
                    p.op("pe", lambda e, Y=Y: e.transpose(pA[:, 3, :], Y[:], ident), reads=[Yk, "cst"], writes=[K("pA")])
                    YT, YTk = YTr.next()
                    p.op("act", lambda e, YT=YT: e.copy(out=YT[:], in_=pA[:, 3, :]), reads=[K("pA")], writes=[YTk])
                    M, Mk = Mr.next()
                    p.op("pool", lambda e, M=M, Y=Y: e.tensor_tensor(out=M[:], in0=Y[:], in1=ident, op=ALU.add), reads=[Yk, "cst"], writes=[Mk])
                    yield
                    for it in range(5):
                        last = (it == 4)
                        if not last:
                            p.op("pe", lambda e, Y=Y, YT=YT: e.matmul(pB[:, 0, :], YT[:], Y[:], start=True, stop=True), reads=[Yk, YTk], writes=[K("pA")])
                        p.op("pe", lambda e, Y=Y, YT=YT: e.matmul(pB[:, 1, :], Y[:], YT[:], start=True, stop=True), reads=[Yk, YTk], writes=[K("pA")])
                        Y2, Y2k = Yr.next()
                        YT2, YT2k = YTr.next()
                        if not last:
                            p.op("dve", lambda e, Y2=Y2: e.tensor_copy(out=Y2[:], in_=pB[:, 0, :]), reads=[K("pA")], writes=[Y2k])
                        p.op("act", lambda e, YT2=YT2: e.copy(out=YT2[:], in_=pB[:, 1, :]), reads=[K("pA")], writes=[YT2k])
                        yield
                        p.op("pe", lambda e, YT2=YT2, M=M: e.matmul(pB[:, 2, :], YT2[:], M[:], start=True, stop=True), reads=[YT2k, Mk], writes=[K("pA")])
                        M2, M2k = Mr.next()
                        p.op("dve", lambda e, M2=M2, M=M: e.tensor_tensor(out=M2[:], in0=pB[:, 2, :], in1=M[:], op=ALU.add), reads=[K("pA"), Mk], writes=[M2k])
                        Y, Yk, YT, YTk, M, Mk = Y2, Y2k, YT2, YT2k, M2, M2k
                        yield
                    Mb, Mbk = Mbr.next()
                    p.op("act", lambda e, Mb=Mb, M=M: e.copy(out=Mb[:], in_=M[:]), reads=[Mk], writes=[Mbk])
                    R, Rk = Rr.next()
                    p.op("dve", lambda e, R=R, n=n, col=col: e.tensor_scalar(out=R[:], in0=ktok[:, n, :], scalar1=col(EGC), scalar2=None, op0=ALU.mult),
                         reads=[K("ktok")] + sck, writes=[Rk])
                    dg, dgk = dgr.next()
                    p.op("act", lambda e, dg=dg, col=col: e.mul(out=dg[:, 0:128], in_=ident, mul=col(BETA)), reads=["cst"] + sck, writes=[(dgk, 0)])
                    p.op("act", lambda e, dg=dg, col=col: e.mul(out=dg[:, 128:256], in_=ident, mul=col(EGC)), reads=["cst"] + sck, writes=[(dgk, 1)])
                    kd, kdk = kdr.next()
                    p.op("act", lambda e, kd=kd, n=n, col=col: e.mul(out=kd[:, 0, :], in_=ktok[:, n, :], mul=col(EKDA)), reads=[K("ktok")] + sck, writes=[(kdk, 0)])
                    p.op("pool", lambda e, kd=kd, n=n, col=col: e.tensor_scalar(out=kd[:, 1, :], in0=ktok[:, n, :], scalar1=col(EKDB), scalar2=None,
                                                                               op0=ALU.mult), reads=[K("ktok")] + sck, writes=[(kdk, 1)])
                    p.op("pe", lambda e, dg=dg: e.matmul(pC[:, 256:512], onesb[:], dg[:], start=True, stop=True), reads=[(dgk, 0), (dgk, 1), "onesb"], writes=[K("pC")])

                    def mmxw(e, Mb=Mb, R=R, n=n):
                        e.matmul(pC[:, 0:128], Mb[:], vtok[:, n, :], start=True, stop=True)
                        return e.matmul(pC[:, 128:256], Mb[:], R[:], start=True, stop=True)
                    p.op("pe", mmxw, reads=[Mbk, Rk, K("vtok")], writes=[K("pC")])
                    p.op("pe", lambda e, Mb=Mb, R=R: e.matmul(pB[:, 3, :], R[:], Mb[:], start=True, stop=True), reads=[Mbk, Rk], writes=[K("pA")])
                    yield
                    bc, bck = bcr.next()
                    p.op("act", lambda e, bc=bc: e.copy(out=bc[:], in_=pC[:, 256:384]), reads=[K("pC")], writes=[bck])
                    qp, qpk = qpr.next()
                    p.op("dve", lambda e, qp=qp, cs=cs: e.tensor_tensor(out=qp[:], in0=qT[:, cs], in1=pC[:, 384:512], op=ALU.mult), reads=[K("qT"), K("pC")], writes=[qpk])
                    u, uk = ur.next()
                    p.op("dve", lambda e, u=u, col=col: e.tensor_scalar(out=u[:], in0=pC[:, 0:128], scalar1=col(BETA), scalar2=None, op0=ALU.mult),
                         reads=[K("pC")] + sck, writes=[uk])
                    wT, wTk = wTr.next()
                    p.op("dve", lambda e, wT=wT, bc=bc: e.tensor_tensor(out=wT[:], in0=pB[:, 3, :], in1=bc[:], op=ALU.mult), reads=[K("pA"), bck], writes=[wTk])
                    yield
                    vn, vnk = vnr.next()
                    o, ok_ = osr.next()
                    for x in range(2):
                        xs = slice(64 * x, 64 * x + 64)
                        p.op("pe", lambda e, wT=wT: e.matmul(pD[:, 0, :], wT[:], Sb[:], start=True, stop=True), reads=[wTk, K("Sb")], writes=[K("pC")])
                        if x == 0:
                            p.op("dve", lambda e, vn=vn, u=u: e.tensor_tensor(out=vn[:], in0=u[:], in1=pD[:, 0, :], op=ALU.subtract), reads=[uk, K("pC")], writes=[vnk])
                        else:
                            p.op("dve", lambda e, vn=vn, u=u: e.tensor_tensor(out=vn[64:128, :], in0=u[64:128, :], in1=pD[64:128, 0, :], op=ALU.subtract),
                                 reads=[uk, K("pC"), vnk], writes=[vnk])
                        yield

                        def mmo(e, qp=qp, QK=QK, vn=vn):
                            e.matmul(pD[:, 1, :], qp[:], Sb[:], start=True, stop=False)
                            return e.matmul(pD[:, 1, :], QK[:], vn[:], start=False, stop=True)
                        p.op("pe", mmo, reads=[qpk, QKk, vnk, K("Sb")], writes=[K("pC")])
                        p.op("pe", lambda e, kd=kd, vn=vn, x=x: e.matmul(pD[:, 2, :], kd[:, x, :], vn[:], start=True, stop=True), reads=[(kdk, x), vnk], writes=[K("pC")])
                        p.op("act", lambda e, o=o, xs=xs: e.copy(out=o[xs, :], in_=pD[xs, 1, :]), reads=[K("pC"), ok_], writes=[ok_])
                        p.op("dve", lambda e, col=col, x=x: e.scalar_tensor_tensor(out=Sst[:], in0=Sst[:], scalar=col(ELA + x), in1=pD[:, 2, :],
                                                                                   op0=ALU.mult, op1=ALU.add), reads=[K("S"), K("pC")] + sck, writes=[K("S")])
                        p.op("act", lambda e: e.copy(out=Sb[:], in_=Sst[:]), reads=[K("S")], writes=[K("Sb")])
                        yield
                    sm, smk = smr.next()
                    ot, otk = otr.next()
                    p.op("act", lambda e, ot=ot, o=o, sm=sm: e.activation(out=ot[:], in_=o[:], func=AF.Square, accum_out=sm[:, 0:1]), reads=[ok_], writes=[otk, (smk, 0)])
                    p.op("act", lambda e, sm=sm: e.activation(out=sm[:, 1:2], in_=sm[:, 0:1], func=AF.Sqrt, bias=ph.eps_ap[:, 0:1], scale=1.0 / 128),
                         reads=[(smk, 0), "eps"], writes=[(smk, 1)])
                    p.op("dve", lambda e, sm=sm: e.reciprocal(out=sm[:, 2:3], in_=sm[:, 1:2]), reads=[(smk, 1)], writes=[(smk, 2)])
                    p.op("dve", lambda e, ot=ot, o=o, sm=sm: e.scalar_tensor_tensor(out=ot[:], in0=o[:], scalar=sm[:, 2:3], in1=gnrow[:], op0=ALU.mult, op1=ALU.mult),
                         reads=[ok_, (smk, 2), "gnrow", otk], writes=[otk])
                    og, ogk = ogr.next()
                    p.op("pool", lambda e, og=og, ot=ot, n=n: e.tensor_tensor(out=og[:], in0=ot[:], in1=ztok[:, n, :], op=ALU.mult), reads=[otk, K("ztok")], writes=[ogk])
                    p.op("pe", lambda e, og=og: e.matmul(pD[:, 3, :], og[:], identb[:], start=True, stop=True), reads=[ogk, "identb"], writes=[K("pC")])
                    p.op("act", lambda e, cs=cs: e.copy(out=oTs[:, cs], in_=pD[:, 3, :]), reads=[K("pC")], writes=[K(("oTs", n))])
                    yield
                p.dma("sp", oT[hv], oTs[:], reads=[K(("oTs", n)) for n in range(16)])

        S_g = ph.sb([128, 16, 32], F32, "gtl")
        p.dma("sp", S_g[:], S["sc"][6], writes=["gtl"])
        interleave([stream(s) for s in range(nstreams)])
```

```python
import numpy as np
from contextlib import ExitStack
import concourse.bass as bass
import concourse.mybir as mybir
from concourse.bass_utils import run_bass_kernel_spmd

F32 = mybir.dt.float32
BF16 = mybir.dt.bfloat16
AF = mybir.ActivationFunctionType
ALU = mybir.AluOpType
AX = mybir.AxisListType

D = 2048
T = 2048
KC = D // 128
MEM = 256
DEPTH = 4
FFN_H = 5632
HC = FFN_H // 128
EPS = 1e-6
ACTIVE_CORES = [0, 1, 4, 5]


class Prog:
    ENGS = ("pe", "act", "dve", "pool", "sp")
    NDMA = 6

    def __init__(self, nc, es, name):
        self.nc = nc
        self.name = name
        self.ops = {e: [] for e in self.ENGS}
        self.cnt = {e: 0 for e in self.ENGS}
        self.sem = {e: nc.alloc_semaphore(f"{name}_{e}") for e in ("pe", "act", "dve", "pool")}
        self.dsem = {q: [nc.alloc_semaphore(f"{name}_{q}d{i}") for i in range(self.NDMA)]
                     for q in ("sp", "pool")}
        self.all_sems = list(self.sem.values()) + [x for q in self.dsem.values() for x in q]
        self.dcnt = {"sp": 0, "pool": 0}
        self.known = {e: {} for e in self.ENGS}
        self.res = {}
        self.excl_names = set()

    @staticmethod
    def _base(key):
        while isinstance(key, tuple):
            key = key[0]
        return key

    def _add_wait(self, eng, waits, ev):
        if ev is None:
            return
        sem, val, src = ev
        if src == "pe" and eng == "pe":
            return
        k = self.known[eng]
        if k.get(sem.name, 0) >= val:
            return
        k[sem.name] = val
        waits.append((sem, val))

    def op(self, eng, fn, reads=(), writes=(), dma=False):
        waits = []
        for r in reads:
            st = self.res.get(r)
            if st is not None:
                self._add_wait(eng, waits, st["w"])
                if self._base(r) in self.excl_names:
                    for ev in st["r"]:
                        if ev[2] != eng:
                            self._add_wait(eng, waits, ev)
        for w in writes:
            st = self.res.get(w)
            if st is not None:
                self._add_wait(eng, waits, st["w"])
                for ev in st["r"]:
                    self._add_wait(eng, waits, ev)
        if dma:
            j = self.dcnt[eng]
            self.dcnt[eng] += 1
            sem = self.dsem[eng][j % self.NDMA]
            val = 16 * (j // self.NDMA + 1)
            if j >= self.NDMA:
                self._add_wait(eng, waits, (sem, val - 16, "dma"))
            ev = (sem, val, "dma")
            inc = (sem, 16)
        else:
            self.cnt[eng] += 1
            ev = (self.sem[eng], self.cnt[eng], eng)
            inc = (self.sem[eng], 1)
        for r in reads:
            st = self.res.setdefault(r, {"w": None, "r": []})
            st["r"].append(ev)
        for w in writes:
            self.res[w] = {"w": ev, "r": []}
        self.ops[eng].append((waits, fn, inc))
        return ev

    def dma(self, q, out, in_, reads=(), writes=(), **kw):
        return self.op(q, lambda e: e.dma_start(out=out, in_=in_, **kw), reads, writes, dma=True)

    def emit(self, block):
        def run(eng_name):
            def body(e):
                for waits, fn, inc in self.ops[eng_name]:
                    for sem, val in waits:
                        e.wait_ge(sem, val)
                    ins = fn(e)
                    ins.then_inc(inc[0], inc[1])
                if eng_name in self.dcnt:
                    n = self.dcnt[eng_name]
                    for i in range(min(n, self.NDMA)):
                        tot = (n - 1 - i) // self.NDMA + 1
                        e.wait_ge(self.dsem[eng_name][i], 16 * tot)
            return body
        block.tensor(run("pe"))
        block.scalar(run("act"))
        block.vector(run("dve"))
        block.gpsimd(run("pool"))
        block.sync(run("sp"))


class Phase:
    def __init__(self, nc, name):
        self.nc = nc
        self.name = name

    def __enter__(self):
        self.es = ExitStack()
        self.es.__enter__()
        self.p = Prog(self.nc, self.es, self.name)
        self._n = 0
        return self

    def sb(self, shape, dt, name=None):
        self._n += 1
        return self.es.enter_context(self.nc.sbuf_tensor(f"{self.name}_{name or 't'}{self._n}", list(shape), dt))

    def ps(self, shape, dt=F32, name=None):
        self._n += 1
        self.p.excl_names.add(name)
        esz = 4 if dt == F32 else 2
        t = self.es.enter_context(self.nc.psum_tensor(f"{self.name}_{name or 'p'}{self._n}", [128, 2048 // esz], dt))
        n = int(np.prod(shape[1:]))
        v = t[:shape[0], :n]
        if len(shape) == 3:
            v = v.rearrange("p (a b) -> p a b", a=shape[1])
        return v

    def __exit__(self, *a):
        if a[0] is None:
            with self.nc.Block() as block:
                self.p.emit(block)
            self.nc.all_engine_barrier()
            self.nc.clear_and_free_semaphores(self.p.all_sems)
            self.nc.all_engine_barrier()
        return self.es.__exit__(*a)


class Ring:
    def __init__(self, ph, n, shape, dt, name, psum=False):
        self.bufs = [(ph.ps(shape, dt, name) if psum else ph.sb(shape, dt, name)) for _ in range(n)]
        self.name = name
        self.i = 0

    def next(self):
        k = self.i % len(self.bufs)
        self.i += 1
        return self.bufs[k], (self.name, k)


def load_consts(ph, C):
    p = ph.p
    c = {}
    c["ones_bf"] = ph.sb([128, 128], BF16, "ones")
    p.op("pool", lambda e: e.memset(c["ones_bf"][:], 1.0), writes=["ones_bf"])
    return c


def rstd_from_sumsq(ph, ps_ss, rstd_sb, key_ps, key_out, n, scale):
    p = ph.p
    p.op("act", lambda e: e.activation(out=rstd_sb[:, :n], in_=ps_ss[:, :n], func=AF.Sqrt, bias=ph.eps_ap[:, 0:1], scale=scale),
         reads=[key_ps, "eps"], writes=[key_out])
    p.op("dve", lambda e: e.reciprocal(out=rstd_sb[:, :n], in_=rstd_sb[:, :n]), reads=[key_out], writes=[key_out])


def make_eps(ph):
    ph.eps_ap = ph.sb([128, 1], F32, "eps")
    ph.p.op("pool", lambda e: e.memset(ph.eps_ap[:], EPS), writes=["eps"])


def norm_residual_phase(nc, name, yT, hT, xnT, g_post, g_pre, h_in=None, final_out=None):
    TT = 512
    NT = T // TT
    with Phase(nc, name) as ph:
        p = ph.p
        make_eps(ph)
        ones = ph.sb([128, 128], BF16, "ones")
        p.op("pool", lambda e: e.memset(ones[:], 1.0), writes=["ones"])
        gp = ph.sb([128, KC], F32, "gpost")
        p.dma("sp", gp[:], g_post, writes=["gp"])
        if g_pre is not None:
            gq = ph.sb([128, KC], F32, "gpre")
            p.dma("sp", gq[:], g_pre, writes=["gq"])
        yr = Ring(ph, 2, [128, KC, TT], F32, "y")
        hr = Ring(ph, 2, [128, KC, TT], F32, "h")
        sqr = Ring(ph, 2, [128, KC, TT], BF16, "sq")
        xr = Ring(ph, 2, [128, KC, TT], BF16, "xn")
        rr = Ring(ph, 2, [128, TT], F32, "rstd")
        pss = Ring(ph, 2, [128, TT], F32, "ss", psum=True)
        h_src = hT if h_in is None else h_in
        for tt in range(NT):
            ts = slice(tt * TT, (tt + 1) * TT)
            y, yk = yr.next()
            h, hk = hr.next()
            sq, sqk = sqr.next()
            rs, rk = rr.next()
            ps, pk = pss.next()
            p.dma("sp", y[:], yT[:, :, ts].rearrange("k p t -> p k t"), writes=[yk])
            p.dma("sp", h[:], h_src[:, :, ts].rearrange("k p t -> p k t"), writes=[hk])
            p.op("act", lambda e, sq=sq, y=y: e.activation(out=sq[:], in_=y[:], func=AF.Square), reads=[yk], writes=[sqk])

            def mm(e, ps=ps, sq=sq):
                for k in range(KC):
                    ins = e.matmul(ps[:], ones[:], sq[:, k, :], start=(k == 0), stop=(k == KC - 1))
                return ins
            p.op("pe", mm, reads=[sqk, "ones"], writes=[pk])
            rstd_from_sumsq(ph, ps, rs, pk, rk, TT, 1.0 / D)
            for k in range(KC):
                p.op("dve", lambda e, k=k, y=y, rs=rs: e.scalar_tensor_tensor(
                    out=y[:, k, :], in0=y[:, k, :], scalar=gp[:, k:k + 1], in1=rs[:], op0=ALU.mult, op1=ALU.mult),
                    reads=[yk, rk, "gp"], writes=[yk])
            p.op("pool", lambda e, y=y, h=h: e.tensor_tensor(out=h[:], in0=h[:], in1=y[:], op=ALU.add), reads=[yk, hk], writes=[hk])
            if final_out is not None:
                p.dma("sp", final_out[:, :, ts].rearrange("k p t -> p k t"), h[:], reads=[hk])
            else:
                p.dma("sp", hT[:, :, ts].rearrange("k p t -> p k t"), h[:], reads=[hk])
            if g_pre is not None:
                sq2, sq2k = sqr.next()
                rs2, r2k = rr.next()
                ps2, p2k = pss.next()
                xn, xk = xr.next()
                p.op("act", lambda e, sq2=sq2, h=h: e.activation(out=sq2[:], in_=h[:], func=AF.Square), reads=[hk], writes=[sq2k])

                def mm2(e, ps2=ps2, sq2=sq2):
                    for k in range(KC):
                        ins = e.matmul(ps2[:], ones[:], sq2[:, k, :], start=(k == 0), stop=(k == KC - 1))
                    return ins
                p.op("pe", mm2, reads=[sq2k, "ones"], writes=[p2k])
                rstd_from_sumsq(ph, ps2, rs2, p2k, r2k, TT, 1.0 / D)
                for k in range(KC):
                    eng = "dve"
                    p.op(eng, lambda e, k=k, h=h, rs2=rs2, xn=xn: e.scalar_tensor_tensor(
                        out=xn[:, k, :], in0=h[:, k, :], scalar=gq[:, k:k + 1], in1=rs2[:], op0=ALU.mult, op1=ALU.mult),
                        reads=[hk, r2k, "gq"], writes=[xk])
                p.dma("sp", xnT[:, :, ts].rearrange("k p t -> p k t"), xn[:], reads=[xk])


def prenorm_phase(nc, name, hT, xnT, g_pre):
    TT = 512
    NT = T // TT
    with Phase(nc, name) as ph:
        p = ph.p
        make_eps(ph)
        ones = ph.sb([128, 128], BF16, "ones")
        p.op("pool", lambda e: e.memset(ones[:], 1.0), writes=["ones"])
        gq = ph.sb([128, KC], F32, "gpre")
        p.dma("sp", gq[:], g_pre, writes=["gq"])
        hr = Ring(ph, 2, [128, KC, TT], F32, "h")
        sqr = Ring(ph, 2, [128, KC, TT], BF16, "sq")
        xr = Ring(ph, 2, [128, KC, TT], BF16, "xn")
        rr = Ring(ph, 2, [128, TT], F32, "rstd")
        pss = Ring(ph, 2, [128, TT], F32, "ss", psum=True)
        for tt in range(NT):
            ts = slice(tt * TT, (tt + 1) * TT)
            h, hk = hr.next()
            sq2, sq2k = sqr.next()
            rs2, r2k = rr.next()
            ps2, p2k = pss.next()
            xn, xk = xr.next()
            p.dma("sp", h[:], hT[:, :, ts].rearrange("k p t -> p k t"), writes=[hk])
            p.op("act", lambda e, sq2=sq2, h=h: e.activation(out=sq2[:], in_=h[:], func=AF.Square), reads=[hk], writes=[sq2k])

            def mm2(e, ps2=ps2, sq2=sq2):
                for k in range(KC):
                    ins = e.matmul(ps2[:], ones[:], sq2[:, k, :], start=(k == 0), stop=(k == KC - 1))
                return ins
            p.op("pe", mm2, reads=[sq2k, "ones"], writes=[p2k])
            rstd_from_sumsq(ph, ps2, rs2, p2k, r2k, TT, 1.0 / D)
            for k in range(KC):
                eng = "dve"
                p.op(eng, lambda e, k=k, h=h, rs2=rs2, xn=xn: e.scalar_tensor_tensor(
                    out=xn[:, k, :], in0=h[:, k, :], scalar=gq[:, k:k + 1], in1=rs2[:], op0=ALU.mult, op1=ALU.mult),
                    reads=[hk, r2k, "gq"], writes=[xk])
            p.dma("sp", xnT[:, :, ts].rearrange("k p t -> p k t"), xn[:], reads=[xk])


def ffn_phase(nc, name, xnT, yT, wgu, wd):
    G = 8
    groups = [list(range(s, min(s + G, HC))) for s in range(0, HC, G)]
    with Phase(nc, name) as ph:
        p = ph.p
        xn = ph.sb([128, KC, T], BF16, "xn")
        for k4 in range(4):
            ks = slice(k4 * 4, k4 * 4 + 4)
            p.dma("sp", xn[:, ks, :], xnT[ks].rearrange("k p t -> p k t"), writes=[("xn", k4)])
        xkeys = [("xn", i) for i in range(4)]
        hid = ph.sb([128, G, T], BF16, "hid")
        wdb = ph.sb([128, G, D], BF16, "wd")
        wr = Ring(ph, 3, [128, 2, KC, 128], BF16, "wgu")
        pgr = Ring(ph, 2, [128, 512], F32, "pg", psum=True)
        pur = Ring(ph, 2, [128, 512], F32, "pu", psum=True)
        pyr = Ring(ph, 2, [128, 512], F32, "py", psum=True)
        sgr = Ring(ph, 2, [128, 512], F32, "sg")
        str_ = Ring(ph, 4, [128, 512], F32, "st")
        ev = 0
        for gi, grp in enumerate(groups):
            for hl, hc in enumerate(grp):
                w, wk = wr.next()
                p.dma("pool", w[:, 0], wgu[hc], writes=[(wk, 0)])
                p.dma("pool", w[:, 1], wgu[HC + hc], writes=[(wk, 1)])
                for tt in range(4):
                    ts = slice(tt * 512, (tt + 1) * 512)
                    pg, pgk = pgr.next()
                    pu, puk = pur.next()

                    def mmg(e, w=w, pg=pg, ts=ts):
                        for k in range(KC):
                            ins = e.matmul(pg[:], w[:, 0, k, :], xn[:, k, ts], start=(k == 0), stop=(k == KC - 1))
                        return ins

                    def mmu(e, w=w, pu=pu, ts=ts):
                        for k in range(KC):
                            ins = e.matmul(pu[:], w[:, 1, k, :], xn[:, k, ts], start=(k == 0), stop=(k == KC - 1))
                        return ins
                    p.op("pe", mmg, reads=[(wk, 0)] + xkeys, writes=[pgk])
                    p.op("pe", mmu, reads=[(wk, 1)] + xkeys, writes=[puk])
                    sg, sgk = sgr.next()
                    p.op("act", lambda e, sg=sg, pg=pg: e.activation(out=sg[:], in_=pg[:], func=AF.Silu), reads=[pgk], writes=[sgk])
                    p.op("dve", lambda e, sg=sg, pu=pu, hl=hl, ts=ts: e.tensor_tensor(out=hid[:, hl, ts], in0=sg[:], in1=pu[:], op=ALU.mult),
                         reads=[sgk, puk], writes=[("hid", hl, tt)])
            for hl, hc in enumerate(grp):
                p.dma("pool", wdb[:, hl, :], wd[hc], writes=[("wd", hl)])
            n = len(grp)
            for fo in range(KC):
                for tt in range(4):
                    ts = slice(tt * 512, (tt + 1) * 512)
                    py, pyk = pyr.next()

                    def mmd(e, py=py, fo=fo, ts=ts, n=n):
                        for hl in range(n):
                            ins = e.matmul(py[:], wdb[:, hl, fo * 128:(fo + 1) * 128], hid[:, hl, ts], start=(hl == 0), stop=(hl == n - 1))
                        return ins
                    p.op("pe", mmd, reads=[("wd", hl) for hl in range(n)] + [("hid", hl, tt) for hl in range(n)], writes=[pyk])
                    st, stk = str_.next()
                    if ev % 2 == 0:
                        p.op("act", lambda e, st=st, py=py: e.copy(out=st[:], in_=py[:]), reads=[pyk], writes=[stk])
                    else:
                        p.op("dve", lambda e, st=st, py=py: e.tensor_copy(out=st[:], in_=py[:]), reads=[pyk], writes=[stk])
                    ev += 1
                    kw = {} if gi == 0 else {"accum_op": ALU.add}
                    p.dma("pool", yT[fo, :, ts], st[:], reads=[stk], writes=[("yT", fo, tt)], **kw)


def xattn_phase(nc, name, xnT, yT, oT, wq, wo, kT_d, v_d):
    NH = 4
    sc = float(512 ** -0.5)
    with Phase(nc, name) as ph:
        p = ph.p
        xn = ph.sb([128, KC, T], BF16, "xn")
        for k4 in range(4):
            ks = slice(k4 * 4, k4 * 4 + 4)
            p.dma("sp", xn[:, ks, :], xnT[ks].rearrange("k p t -> p k t"), writes=[("xn", k4)])
        xkeys = [("xn", i) for i in range(4)]
        kT = ph.sb([128, KC, MEM], BF16, "kT")
        vv = ph.sb([128, 2, D], BF16, "v")
        p.dma("sp", kT[:], kT_d, writes=["kT"])
        p.dma("sp", vv[:], v_d, writes=["v"])
        ident = ph.sb([128, 128], BF16, "ident")
        p.dma("pool", ident[:], ph.ident_d, writes=["ident"])
        wr = Ring(ph, 3, [128, KC, 128], BF16, "w")
        qh = ph.sb([128, 4, T], BF16, "qh")
        pT = ph.sb([128, 2, T], BF16, "pT")
        ppr = Ring(ph, 2, [128, 512], F32, "pp", psum=True)
        psr = Ring(ph, 2, [128, MEM], F32, "psc", psum=True)
        ptr = Ring(ph, 2, [128, 2, 128], BF16, "ptr", psum=True)
        er = Ring(ph, 2, [128, MEM], F32, "e")
        pbr = Ring(ph, 2, [128, MEM], BF16, "pb")
        smr = Ring(ph, 4, [128, 4], F32, "sm")
        osr = Ring(ph, 3, [128, 512], BF16, "os")
        ev = 0
        for hd in range(NH):
            for c in range(4):
                fo = 4 * hd + c
                w, wk = wr.next()
                p.dma("pool", w[:], wq[fo], writes=[wk])
                for tt in range(4):
                    ts = slice(tt * 512, (tt + 1) * 512)
                    pp, ppk = ppr.next()

                    def mmq(e, w=w, pp=pp, ts=ts):
                        for k in range(KC):
                            ins = e.matmul(pp[:], w[:, k, :], xn[:, k, ts], start=(k == 0), stop=(k == KC - 1))
                        return ins
                    p.op("pe", mmq, reads=[wk] + xkeys, writes=[ppk])
                    if ev % 2 == 0:
                        p.op("act", lambda e, pp=pp, c=c, ts=ts: e.copy(out=qh[:, c, ts], in_=pp[:]), reads=[ppk], writes=[("qh", c, tt)])
                    else:
                        p.op("dve", lambda e, pp=pp, c=c, ts=ts: e.tensor_copy(out=qh[:, c, ts], in_=pp[:]), reads=[ppk], writes=[("qh", c, tt)])
                    ev += 1
            for t16 in range(16):
                tsl = slice(t16 * 128, (t16 + 1) * 128)
                tt = t16 // 4
                psc, pck = psr.next()

                def mms(e, psc=psc, tsl=tsl, hd=hd):
                    for c in range(4):
                        ins = e.matmul(psc[:], qh[:, c, tsl], kT[:, 4 * hd + c, :], start=(c == 0), stop=(c == 3))
                    return ins
                p.op("pe", mms, reads=[("qh", c, tt) for c in range(4)] + ["kT"], writes=[pck])
                sm, smk = smr.next()
                p.op("dve", lambda e, sm=sm, psc=psc: e.reduce_max(out=sm[:, 0:1], in_=psc[:], axis=AX.X), reads=[pck], writes=[(smk, 0)])
                p.op("dve", lambda e, sm=sm: e.tensor_scalar(out=sm[:, 1:2], in0=sm[:, 0:1], scalar1=-sc, scalar2=None, op0=ALU.mult),
                     reads=[(smk, 0)], writes=[(smk, 1)])
                ee, ek = er.next()
                p.op("act", lambda e, ee=ee, psc=psc, sm=sm: e.activation(out=ee[:], in_=psc[:], func=AF.Exp, bias=sm[:, 1:2], scale=sc, accum_out=sm[:, 2:3]),
                     reads=[pck, (smk, 1)], writes=[ek, (smk, 2)])
                p.op("dve", lambda e, sm=sm: e.reciprocal(out=sm[:, 3:4], in_=sm[:, 2:3]), reads=[(smk, 2)], writes=[(smk, 3)])
                pb, pbk = pbr.next()
                p.op("dve", lambda e, pb=pb, ee=ee, sm=sm: e.tensor_scalar(out=pb[:], in0=ee[:], scalar1=sm[:, 3:4], scalar2=None, op0=ALU.mult),
                     reads=[ek, (smk, 3)], writes=[pbk])
                pt, ptk = ptr.next()

                def mmt(e, pt=pt, pb=pb):
                    for mc in range(2):
                        ins = e.transpose(pt[:, mc, :], pb[:, mc * 128:(mc + 1) * 128], ident[:])
                    return ins
                p.op("pe", mmt, reads=[pbk, "ident"], writes=[ptk])
                p.op("act", lambda e, pt=pt, tsl=tsl: e.copy(out=pT[:, :, tsl], in_=pt[:]), reads=[ptk], writes=[("pT", t16)])
            for c in range(4):
                for tt in range(4):
                    ts = slice(tt * 512, (tt + 1) * 512)
                    pp, ppk = ppr.next()

                    def mmo(e, pp=pp, c=c, ts=ts, hd=hd):
                        for mc in range(2):
                            col = hd * 512 + c * 128
                            ins = e.matmul(pp[:], vv[:, mc, col:col + 128], pT[:, mc, ts], start=(mc == 0), stop=(mc == 1))
                        return ins
                    p.op("pe", mmo, reads=["v"] + [("pT", 4 * tt + i) for i in range(4)], writes=[ppk])
                    os_, osk = osr.next()
                    p.op("dve", lambda e, os_=os_, pp=pp: e.tensor_copy(out=os_[:], in_=pp[:]), reads=[ppk], writes=[osk])
                    p.dma("sp", oT[4 * hd + c, :, ts], os_[:], reads=[osk], writes=[("oT", 4 * hd + c, tt)])
        for k4 in range(4):
            ks = slice(k4 * 4, k4 * 4 + 4)
            p.dma("sp", xn[:, ks, :], oT[ks].rearrange("k p t -> p k t"),
                  reads=[("oT", k, tt) for k in range(k4 * 4, k4 * 4 + 4) for tt in range(4)], writes=[("xn", k4)])
        out_proj(ph, xn, xkeys, KC, wo, yT, wr, ppr)


def out_proj(ph, act_sb, act_keys, kc_n, w_d, yT, wr, ppr):
    p = ph.p
    str_ = Ring(ph, 3, [128, 512], F32, "yst")
    ev = 0
    for fo in range(KC):
        w, wk = wr.next()
        p.dma("pool", w[:, :kc_n, :], w_d[fo], writes=[wk])
        for tt in range(4):
            ts = slice(tt * 512, (tt + 1) * 512)
            pp, ppk = ppr.next()

            def mm(e, w=w, pp=pp, ts=ts):
                for k in range(kc_n):
                    ins = e.matmul(pp[:], w[:, k, :], act_sb[:, k, ts], start=(k == 0), stop=(k == kc_n - 1))
                return ins
            p.op("pe", mm, reads=[wk] + list(act_keys), writes=[ppk])
            st, stk = str_.next()
            if ev % 2 == 0:
                p.op("act", lambda e, st=st, pp=pp: e.copy(out=st[:], in_=pp[:]), reads=[ppk], writes=[stk])
            else:
                p.op("dve", lambda e, st=st, pp=pp: e.tensor_copy(out=st[:], in_=pp[:]), reads=[ppk], writes=[stk])
            ev += 1
            p.dma("sp", yT[fo, :, ts], st[:], reads=[stk], writes=[("yT", fo, tt)])


def memkv_phase(nc, name, memT, g_mem, wkv, kT_d, v_d):
    with Phase(nc, name) as ph:
        p = ph.p
        make_eps(ph)
        ones = ph.sb([128, 128], BF16, "ones")
        p.op("pool", lambda e: e.memset(ones[:], 1.0), writes=["ones"])
        g = ph.sb([128, KC], F32, "g")
        p.dma("sp", g[:], g_mem, writes=["g"])
        m = ph.sb([128, KC, MEM], F32, "m")
        p.dma("sp", m[:], memT.rearrange("k p t -> p k t"), writes=["m"])
        sq = ph.sb([128, KC, MEM], BF16, "sq")
        p.op("act", lambda e: e.activation(out=sq[:], in_=m[:], func=AF.Square), reads=["m"], writes=["sq"])
        ps = ph.ps([128, MEM], F32, "ss")

        def mm(e):
            for k in range(KC):
                ins = e.matmul(ps[:], ones[:], sq[:, k, :], start=(k == 0), stop=(k == KC - 1))
            return ins
        p.op("pe", mm, reads=["sq", "ones"], writes=["ps"])
        rs = ph.sb([128, MEM], F32, "rs")
        rstd_from_sumsq(ph, ps, rs, "ps", "rs", MEM, 1.0 / D)
        mn = ph.sb([128, KC, MEM], BF16, "mn")
        for k in range(KC):
            p.op("dve", lambda e, k=k: e.scalar_tensor_tensor(out=mn[:, k, :], in0=m[:, k, :], scalar=g[:, k:k + 1], in1=rs[:],
                                                              op0=ALU.mult, op1=ALU.mult), reads=["m", "rs", "g"], writes=[("mn", k)])
        mkeys = [("mn", k) for k in range(KC)]
        wr = Ring(ph, 3, [128, KC, 128], BF16, "w")
        ppr = Ring(ph, 2, [128, MEM], F32, "pp", psum=True)
        kT = ph.sb([128, KC, MEM], BF16, "kT")
        vv = ph.sb([128, 2, D], BF16, "v")
        for fo in range(KC):
            w, wk = wr.next()
            p.dma("pool", w[:], wkv[fo], writes=[wk])
            pp, ppk = ppr.next()

            def mmk(e, w=w, pp=pp):
                for k in range(KC):
                    ins = e.matmul(pp[:], w[:, k, :], mn[:, k, :], start=(k == 0), stop=(k == KC - 1))
                return ins
            p.op("pe", mmk, reads=[wk] + mkeys, writes=[ppk])
            p.op("act", lambda e, pp=pp, fo=fo: e.copy(out=kT[:, fo, :], in_=pp[:]), reads=[ppk], writes=[("kT", fo)])
        for fo in range(KC):
            w, wk = wr.next()
            p.dma("pool", w[:], wkv[KC + fo], writes=[wk])
            pp, ppk = ppr.next()

            def mmv(e, w=w, pp=pp):
                for mc in range(2):
                    for k in range(KC):
                        ins = e.matmul(pp[:, mc * 128:(mc + 1) * 128], mn[:, k, mc * 128:(mc + 1) * 128], w[:, k, :],
                                       start=(k == 0), stop=(k == KC - 1))
                return ins
            p.op("pe", mmv, reads=[wk] + mkeys, writes=[ppk])
            p.op("dve", lambda e, pp=pp, fo=fo: e.tensor_copy(out=vv[:, :, fo * 128:(fo + 1) * 128],
                                                             in_=pp[:].rearrange("p (m c) -> p m c", m=2)),
                 reads=[ppk], writes=[("v", fo)])
        p.dma("sp", kT_d, kT[:], reads=[("kT", fo) for fo in range(KC)])
        p.dma("sp", v_d, vv[:], reads=[("v", fo) for fo in range(KC)])


def fm_tiles(W):
    Din, Fo = W.shape
    return np.ascontiguousarray(W.reshape(Din // 128, 128, Fo // 128, 128).transpose(2, 1, 0, 3))


def gain_layout(g):
    return np.ascontiguousarray(g.reshape(-1, 128).T)


def plan_maps(plan):
    ls = sorted(set(plan["layers"]))
    lmap = {l: i for i, l in enumerate(ls)}
    ev = sorted(set(l // 2 for l in ls if l % 2 == 0 and "mix" in plan["subs"]))
    od = sorted(set(l // 2 for l in ls if l % 2 == 1 and "mix" in plan["subs"]))
    return ls, lmap, ev, {j: i for i, j in enumerate(ev)}, od, {j: i for i, j in enumerate(od)}


def build_program(plan, debug=()):
    nc = bass.Bass("TRN2", target_bir_lowering=False)
    I = {}
    ls, lmap, ev, emap, od, omap = plan_maps(plan)
    NL, NE, NO = len(ls), max(len(ev), 1), max(len(od), 1)
    has_x, has_f = "xattn" in plan["subs"], "ffn" in plan["subs"]

    def inp(name, shape, dt=F32):
        I[name] = nc.dram_tensor(name, list(shape), dt, kind="ExternalInput").ap()
        return I[name]

    xT = inp("xT", [KC, 128, T])
    memT = inp("memT", [KC, 128, MEM])
    ident = inp("ident", [128, 128])
    g_mem = inp("g_mem", [128, KC])
    wkv = inp("wkv", [32, 128, KC, 128])
    gains = inp("gains", [NL, 6, 128, KC])
    wq = inp("wq", [NL, KC, 128, KC, 128] if has_x else [1, 1, 128, KC, 128])
    wo = inp("wo", [NL, KC, 128, KC, 128] if has_x else [1, 1, 128, KC, 128])
    wgu = inp("wgu", [NL, 2 * HC, 128, KC, 128] if has_f else [1, 1, 128, KC, 128])
    wd = inp("wd", [NL, HC, 128, D] if has_f else [1, 1, 128, D])
    ab_in = inp("ab_in", [NE, 48, 128, KC, 128] if ev else [1, 1, 128, KC, 128])
    ab_lr = inp("ab_lr", [NE, 128, KC, 16])
    ab_w2b = inp("ab_w2b", [NE, 17, 512])
    ab_gn = inp("ab_gn", [NE, 128, 2])
    ab_out = inp("ab_out", [NE, KC, 128, KC, 128] if ev else [1, 1, 128, KC, 128])
    sbc = inp("sbc", [128, 20, 128])
    gd_in = inp("gd_in", [NO, 96, 128, KC, 128] if od else [1, 1, 128, KC, 128])
    gd_ba = inp("gd_ba", [NO, 128, KC, 64])
    gd_cw = inp("gd_cw", [NO, 64, 128, 4])
    gd_alog = inp("gd_alog", [NO, 128, 4, 32])
    gd_dtb = inp("gd_dtb", [NO, 128, 4, 32])
    gd_gn = inp("gd_gn", [NO, 128, 128])
    gd_out = inp("gd_out", [NO, KC, 128, 32, 128] if od else [1, 1, 128, 32, 128])
    gdc = inp("gdc", [128, 7, 128])
    outT = nc.dram_tensor("outT", [KC, 128, T], F32, kind="ExternalOutput").ap()
    hT = nc.dram_tensor("hT", [KC, 128, T], F32).ap()
    yT = nc.dram_tensor("yT", [KC, 128, T], F32).ap()
    xnT = nc.dram_tensor("xnT", [KC, 128, T], BF16).ap()
    oT = nc.dram_tensor("oT", [32, 128, T], BF16).ap()
    kT_d = nc.dram_tensor("kT_d", [128, KC, MEM], BF16).ap()
    v_d = nc.dram_tensor("v_d", [128, 2, D], BF16).ap()
    S = {
        "sc": nc.dram_tensor("g_sc", [9, 128, 16, 32], F32).ap(),
        "qk": nc.dram_tensor("g_qk", [16, 2, 128, T], BF16).ap(),
        "ktok": nc.dram_tensor("g_ktok", [16, 128, 16, 128], BF16).ap(),
        "vtok": nc.dram_tensor("g_vtok", [32, 128, 16, 128], BF16).ap(),
        "ztok": nc.dram_tensor("g_ztok", [32, 128, 16, 128], BF16).ap(),
    }
    Phase.ident_d = ident

    memkv_phase(nc, "mkv", memT, g_mem, wkv, kT_d, v_d)
    steps = []
    for layer in plan["layers"]:
        for sub in plan["subs"]:
            steps.append((layer, sub))
    first = True
    for i, (layer, sub) in enumerate(steps):
        gi = {"mix": 0, "xattn": 2, "ffn": 4}[sub]
        nm = f"L{layer}{sub[0]}"
        li = lmap[layer]
        j = emap.get(layer // 2, 0) if layer % 2 == 0 else omap.get(layer // 2, 0)
        if first:
            prenorm_phase(nc, nm + "pn", xT, xnT, gains[li, gi])
        if sub == "xattn":
            xattn_phase(nc, nm, xnT, yT, oT, wq[li], wo[li], kT_d, v_d)
        elif sub == "ffn":
            ffn_phase(nc, nm, xnT, yT, wgu[li], wd[li])
        elif layer % 2 == 0:
            parts = plan.get("parts", "sgo")
            if "s" in parts:
                sb_phase(nc, nm + "s", xnT, oT, ab_in[j], sbc)
            if "g" in parts:
                gla_phase(nc, nm + "g", xnT, oT, ab_in[j], ab_lr[j], ab_w2b[j], ab_gn[j], sbc)
            if "o" in parts:
                outproj_phase(nc, nm + "o", oT, yT, ab_out[j], KC)
        else:
            parts = plan.get("parts", "pco")
            if "p" in parts:
                gdn_prep_phase(nc, nm + "p", xnT, gd_in[j], gd_ba[j], gd_cw[j], gd_alog[j], gd_dtb[j], gdc, S)
            if "c" in parts:
                gdn_core_phase(nc, nm + "c", oT, gd_gn[j], gdc, S)
            if "o" in parts:
                outproj_phase(nc, nm + "o", oT, yT, gd_out[j], 32)
        last = (i == len(steps) - 1)
        if last:
            g_next = None
        else:
            nl, ns = steps[i + 1]
            g_next = gains[lmap[nl], {"mix": 0, "xattn": 2, "ffn": 4}[ns]]
        norm_residual_phase(nc, nm + "nr", yT, hT, xnT, gains[li, gi + 1], g_next,
                            h_in=(xT if first else None), final_out=(outT if last else None))
        first = False
    return nc


def prep_inputs(inputs, b):
    f = lambda a: np.ascontiguousarray(np.asarray(a, dtype=np.float32))
    m = {}
    m["xT"] = f(inputs["x"][b].T).reshape(KC, 128, T)
    m["memT"] = f(inputs["mem"][b].T).reshape(KC, 128, MEM)
    return m


def _tile_cols(W):
    Din, n = W.shape
    return np.ascontiguousarray(W.reshape(Din // 128, 128, n).transpose(1, 0, 2))


def prep_shared(inputs, plan):
    f = lambda a: np.asarray(a, dtype=np.float32)
    ls, lmap, ev, emap, od, omap = plan_maps(plan)
    has_x, has_f = "xattn" in plan["subs"], "ffn" in plan["subs"]
    lx = ls if has_x else ls[:1]
    lf = ls if has_f else ls[:1]
    has_e, has_o = bool(ev), bool(od)
    ev = ev or [0]
    od = od or [0]
    s = {}
    s["ident"] = np.eye(128, dtype=np.float32)
    s["g_mem"] = gain_layout(f(inputs["mem_norm_g"]))
    s["wkv"] = fm_tiles(f(inputs["mem_w_kv"]))
    names = ["mix_pre_g", "mix_post_g", "xattn_pre_g", "xattn_post_g", "ffn_pre_g", "ffn_post_g"]
    s["gains"] = np.stack([np.stack([gain_layout(f(inputs[n])[l]) for n in names]) for l in ls])
    s["wq"] = np.stack([fm_tiles(f(inputs["xattn_w_q"][l])) for l in lx])
    s["wo"] = np.stack([fm_tiles(f(inputs["xattn_w_o"][l])) for l in lx])
    s["wgu"] = np.stack([fm_tiles(f(inputs["ffn_w_gate_up"][l])) for l in lf])
    s["wd"] = np.stack([np.ascontiguousarray(f(inputs["ffn_w_down"][l]).reshape(HC, 128, D)) for l in lf])
    abw = inputs["ab_w_in"]
    s["ab_in"] = np.stack([fm_tiles(f(abw[j])[:, :6144]) for j in ev])
    s["ab_lr"] = np.stack([_tile_cols(f(abw[j])[:, 6144:6160]) for j in ev])
    s["ab_w2b"] = np.stack([np.concatenate([f(inputs["gla_gate_w2"])[j], f(inputs["gla_gate_b"])[j][None, :]], 0) for j in ev])
    s["ab_gn"] = np.stack([np.ascontiguousarray(f(inputs["gla_norm_g"])[j].reshape(2, 128).T) for j in ev])
    s["ab_out"] = np.stack([fm_tiles(f(inputs["ab_w_out"][j])) for j in ev])
    jj = np.arange(128)[:, None]
    ss = np.arange(128)[None, :]
    c = np.zeros((128, 20, 128), np.float32)
    c[:, 0] = jj >= ss
    c[:, 1] = jj < ss
    c[:, 2] = jj <= ss
    c[:, 3] = jj > ss
    t512 = np.arange(512)[None, :]
    for v in range(4):
        c[:, 4 + 4 * v:8 + 4 * v, :] = ((jj + 128 * v) < t512).astype(np.float32).reshape(128, 4, 128)
    s["sbc"] = c
    gw = inputs["gdn_w_in"]
    s["gd_in"] = np.stack([fm_tiles(f(gw[j])[:, :12288]) for j in od])
    s["gd_ba"] = np.stack([_tile_cols(f(gw[j])[:, 12288:12352]) for j in od])
    s["gd_cw"] = np.stack([np.ascontiguousarray(f(inputs["gdn_conv_w"])[j].T.reshape(64, 128, 4)) for j in od])
    s["gd_alog"] = np.stack([np.ascontiguousarray(np.broadcast_to(f(inputs["gdn_a_log"])[j], (128, 4, 32))) for j in od])
    s["gd_dtb"] = np.stack([np.ascontiguousarray(np.broadcast_to(f(inputs["gdn_dt_bias"])[j], (128, 4, 32))) for j in od])
    s["gd_gn"] = np.stack([np.ascontiguousarray(np.broadcast_to(f(inputs["gdn_norm_g"])[j], (128, 128))) for j in od])
    s["gd_out"] = np.stack([fm_tiles(f(inputs["gdn_w_out"][j])) for j in od])
    same = (jj // 64) == (ss // 64)
    g = np.zeros((128, 7, 128), np.float32)
    g[:, 0] = np.eye(128)
    g[:, 1] = same & (jj <= ss)
    g[:, 2] = same & (jj > ss)
    g[:, 3] = same & (jj < ss)
    g[:, 4] = (jj < 64) & (ss >= 0)
    g[:, 5] = (jj >= 64) & (ss >= 0)
    g[:, 6] = 1.0
    s["gdc"] = g
    if not has_x:
        s["wq"] = s["wq"][:, :1]
        s["wo"] = s["wo"][:, :1]
    if not has_f:
        s["wgu"] = s["wgu"][:, :1]
        s["wd"] = s["wd"][:, :1]
    if not has_e:
        s["ab_in"] = s["ab_in"][:, :1]
        s["ab_out"] = s["ab_out"][:, :1]
    if not has_o:
        s["gd_in"] = s["gd_in"][:, :1]
        s["gd_out"] = s["gd_out"][:, :1]
    return {k: np.ascontiguousarray(v) for k, v in s.items()}


def run(inputs, plan):
    import time
    t0 = time.time()
    nc = build_program(plan)
    t1 = time.time()
    shared = prep_shared(inputs, plan)
    zero = {k: np.zeros_like(v) for k, v in prep_inputs(inputs, 0).items()}
    zero.update({k: np.zeros_like(v) for k, v in shared.items()})
    in_maps = []
    for c in range(8):
        if c in ACTIVE_CORES:
            m = prep_inputs(inputs, ACTIVE_CORES.index(c))
            m.update(shared)
        else:
            m = zero
        in_maps.append(m)
    t2 = time.time()
    res = run_bass_kernel_spmd(nc, in_maps, core_ids=list(range(8)))
    t3 = time.time()
    print(f"[kernel] build {t1 - t0:.1f}s prep {t2 - t1:.1f}s launch {t3 - t2:.1f}s", flush=True)
    out = np.stack([res.results[c]["outT"].reshape(D, T).T for c in ACTIVE_CORES])
    return np.ascontiguousarray(out.astype(np.float32))


def kernel(**inputs):
    return run(inputs, {"layers": list(range(DEPTH)), "subs": ["mix", "xattn", "ffn"]})


def interleave(gens):
    gens = list(gens)
    while gens:
        for g in list(gens):
            try:
                next(g)
            except StopIteration:
                gens.remove(g)


def load_xn(ph, xnT):
    p = ph.p
    xn = ph.sb([128, KC, T], BF16, "xn")
    for k4 in range(4):
        ks = slice(k4 * 4, k4 * 4 + 4)
        p.dma("sp", xn[:, ks, :], xnT[ks].rearrange("k p t -> p k t"), writes=[("xn", k4)])
    return xn, [("xn", i) for i in range(4)]


def proj_fm(ph, w, wk, xn, xkeys, ppr, evac, m=128):
    p = ph.p
    for tt in range(4):
        ts = slice(tt * 512, (tt + 1) * 512)
        pp, ppk = ppr.next()

        def mm(e, pp=pp, ts=ts):
            for k in range(KC):
                ins = e.matmul(pp[:m, :], w[:, k, :m], xn[:, k, ts], start=(k == 0), stop=(k == KC - 1))
            return ins
        p.op("pe", mm, reads=[wk] + xkeys, writes=[ppk])
        evac(tt, pp, ppk)


def proj_tm(ph, w, wk, xn, xkeys, ppr, evac, n=128):
    p = ph.p
    for t4 in range(4):
        pp, ppk = ppr.next()

        def mm(e, pp=pp, t4=t4):
            for i in range(4):
                t16 = t4 * 4 + i
                for k in range(KC):
                    ins = e.matmul(pp[:, i * n:(i + 1) * n], xn[:, k, t16 * 128:(t16 + 1) * 128], w[:, k, :n],
                                   start=(k == 0), stop=(k == KC - 1))
            return ins
        p.op("pe", mm, reads=[wk] + xkeys, writes=[ppk])
        evac(t4, pp, ppk)


def sb_phase(nc, name, xnT, oT, win, consts):
    scale = float(128 ** -0.5)
    with Phase(nc, name) as ph:
        p = ph.p
        xn, xkeys = load_xn(ph, xnT)
        cst = ph.sb([128, 20, 128], F32, "cst")
        p.dma("sp", cst[:], consts, writes=["cst"])
        GE, LT = cst[:, 0, :], cst[:, 1, :]
        wr = Ring(ph, 3, [128, KC, 128], BF16, "w")
        ppr = Ring(ph, 2, [128, 512], F32, "pp", psum=True)

        def head_stream(h, sid):
            qT = ph.sb([128, T], BF16, "qT")
            kT = ph.sb([128, T], BF16, "kT")
            vt = ph.sb([128, 16, 128], BF16, "vt")
            pz = ph.ps([128, 512], F32, "pz")
            pA = ph.ps([128, 512], F32, "pA")
            po = ph.ps([128, 512], F32, "po")
            er = Ring(ph, 2, [128, 512], F32, f"e{sid}")
            spr = Ring(ph, 2, [128, 512], F32, f"sp{sid}")
            xr = Ring(ph, 2, [128, 512], F32, f"x{sid}")
            wwr = Ring(ph, 2, [128, 512], BF16, f"ww{sid}")
            osr = Ring(ph, 2, [128, 512], BF16, f"os{sid}")
            K = lambda s: (s, sid)
            while h is not None:
                w, wk = wr.next()
                p.dma("pool", w[:], win[h], writes=[wk])
                proj_fm(ph, w, wk, xn, xkeys, ppr, lambda tt, pp, ppk: p.op(
                    "act", lambda e: e.activation(out=qT[:, tt * 512:(tt + 1) * 512], in_=pp[:], func=AF.Copy, scale=scale),
                    reads=[ppk], writes=[K(("qT", tt))]))
                yield
                w, wk = wr.next()
                p.dma("pool", w[:], win[8 + h], writes=[wk])
                proj_fm(ph, w, wk, xn, xkeys, ppr, lambda tt, pp, ppk: p.op(
                    "dve", lambda e: e.tensor_copy(out=kT[:, tt * 512:(tt + 1) * 512], in_=pp[:]),
                    reads=[ppk], writes=[K(("kT", tt))]))
                yield
                w, wk = wr.next()
                p.dma("pool", w[:], win[16 + h], writes=[wk])
                proj_tm(ph, w, wk, xn, xkeys, ppr, lambda t4, pp, ppk: p.op(
                    "act", lambda e: e.copy(out=vt[:, t4 * 4:(t4 + 1) * 4, :], in_=pp[:].rearrange("p (a b) -> p a b", a=4)),
                    reads=[ppk], writes=[K(("vt", t4))]))
                yield
                for qsb in range(4):
                    qs = slice(qsb * 512, (qsb + 1) * 512)
                    nkb = 4 * qsb + 4
                    for i, kb in enumerate(range(nkb - 1, -1, -1)):
                        ks = slice(kb * 128, (kb + 1) * 128)
                        p.op("pe", lambda e, ks=ks, qs=qs: e.matmul(pz[:], kT[:, ks], qT[:, qs], start=True, stop=True),
                             reads=[K(("kT", kb // 4)), K(("qT", qsb))], writes=[K("pz")])
                        ee, ek = er.next()
                        p.op("act", lambda e, ee=ee: e.activation(out=ee[:], in_=pz[:], func=AF.Exp), reads=[K("pz")], writes=[ek])
                        var = kb - 4 * qsb
                        if var >= 0:
                            p.op("pool", lambda e, ee=ee, var=var: e.tensor_tensor(
                                out=ee[:], in0=ee[:], in1=cst[:, 4 + 4 * var:8 + 4 * var, :].rearrange("p a b -> p (a b)"), op=ALU.mult),
                                reads=[ek, "cst"], writes=[ek])
                        yield
                        sp, spk = spr.next()
                        p.op("act", lambda e, ee=ee, sp=sp: e.activation(out=sp[:], in_=ee[:], func=AF.Ln, bias=1.0, scale=1.0),
                             reads=[ek], writes=[spk])
                        p.op("pe", lambda e, sp=sp, i=i: e.matmul(pA[:], GE, sp[:], start=(i == 0), stop=False, skip_group_check=True),
                             reads=[spk, "cst"], writes=[K("pA")])
                        yield
                        xx, xk = xr.next()
                        p.op("act", lambda e, xx=xx: e.activation(out=xx[:], in_=pA[:], func=AF.Exp, scale=-1.0), reads=[K("pA")], writes=[xk])
                        ww, wwk = wwr.next()
                        p.op("dve", lambda e, ww=ww, ee=ee, xx=xx: e.tensor_tensor(out=ww[:], in0=ee[:], in1=xx[:], op=ALU.mult),
                             reads=[ek, xk], writes=[wwk])
                        p.op("pe", lambda e, sp=sp, i=i, nkb=nkb: e.matmul(pA[:], LT, sp[:], start=False, stop=(i == nkb - 1), skip_group_check=True),
                             reads=[spk, "cst", xk], writes=[K("pA")])
                        yield
                        p.op("pe", lambda e, ww=ww, kb=kb, i=i, nkb=nkb: e.matmul(po[:], vt[:, kb, :], ww[:], start=(i == 0), stop=(i == nkb - 1)),
                             reads=[wwk, K(("vt", kb // 4))], writes=[K("pD")])
                    os_, osk = osr.next()
                    p.op("dve", lambda e, os_=os_: e.tensor_copy(out=os_[:], in_=po[:]), reads=[K("pD")], writes=[osk])
                    p.dma("sp", oT[h, :, qs], os_[:], reads=[osk])
                    yield
                h = h + 2 if h + 2 < 8 else None

        interleave([head_stream(0, 0), head_stream(1, 1)])


def gla_phase(nc, name, xnT, oT, win, wlr, w2b, gn_d, consts):
    qscale = float(128 ** -0.5)
    with Phase(nc, name) as ph:
        p = ph.p
        make_eps(ph)
        xn, xkeys = load_xn(ph, xnT)
        cst = ph.sb([128, 4, 128], F32, "cst")
        p.dma("sp", cst[:], consts[:, 0:4, :], writes=["cst"])
        LE, GT = cst[:, 2, :], cst[:, 3, :]
        ones = ph.sb([128, 128], BF16, "ones")
        p.op("pool", lambda e: e.memset(ones[:], 1.0), writes=["ones"])
        gn = ph.sb([128, 2], F32, "gn")
        p.dma("sp", gn[:], gn_d, writes=["gn"])
        w2 = ph.sb([17, 512], F32, "w2")
        p.dma("sp", w2[:], w2b, writes=["w2"])
        wl = ph.sb([128, KC, 16], BF16, "wl")
        p.dma("pool", wl[:], wlr, writes=["wl"])
        wr = Ring(ph, 3, [128, KC, 128], BF16, "w")
        ppr = Ring(ph, 2, [128, 512], F32, "pp", psum=True)
        pss = ph.ps([128, 512], F32, "pss")
        glr = ph.sb([17, T], F32, "glr")
        p.op("pool", lambda e: e.memset(glr[:], 1.0), writes=["glr"])
        proj_fm(ph, wl, "wl", xn, xkeys, ppr, lambda tt, pp, ppk: p.op(
            "act", lambda e: e.copy(out=glr[0:16, tt * 512:(tt + 1) * 512], in_=pp[0:16, :]), reads=[ppk, "glr"], writes=["glr"]), m=16)

        import os
        STOP = int(os.environ.get("GLA_STOP", "9"))
        CUT = int(os.environ.get("GLA_CUT", "9"))

        def head_stream(h, sid):
            K = lambda s: (s, sid)
            if STOP <= 1:
                return
            qT = ph.sb([128, T], BF16, "qT")
            kT = ph.sb([128, T], BF16, "kT")
            kt = ph.sb([128, 16, 128], BF16, "kt")
            vt = ph.sb([128, 16, 256], BF16, "vt")
            sr = ph.sb([128, 2, T], BF16, "sr")
            spt = ph.sb([128, 16, 128], F32, "spt")
            og = ph.sb([128, 2, T], F32, "og")
            sq = ph.sb([128, 2, 512], BF16, "sq")
            S = ph.sb([128, 256], F32, "S")
            Sb = ph.sb([128, 256], BF16, "Sb")
            pa = ph.ps([128, 4, 128], F32, "pa")
            pb = ph.ps([128, 512], F32, "pb")
            t1r = Ring(ph, 2, [128, 512], F32, f"t1{sid}")
            Er = Ring(ph, 2, [128, 2, 128], F32, f"E{sid}")
            qdr = Ring(ph, 2, [128, 128], BF16, f"qd{sid}")
            kir = Ring(ph, 2, [128, 128], BF16, f"ki{sid}")
            dkr = Ring(ph, 2, [128, 128], F32, f"dk{sid}")
            kdr = Ring(ph, 2, [128, 128], BF16, f"kd{sid}")
            STr = Ring(ph, 2, [128, 128], BF16, f"ST{sid}")
            rsr = Ring(ph, 2, [128, 512], F32, f"rs{sid}")
            o1r = Ring(ph, 2, [128, 512], F32, f"o1{sid}")
            obr = Ring(ph, 2, [128, 512], BF16, f"ob{sid}")
            while h is not None:
                def ld(tile):
                    w, wk = wr.next()
                    p.dma("pool", w[:], win[tile], writes=[wk])
                    return w, wk
                w, wk = ld(24 + h)
                proj_fm(ph, w, wk, xn, xkeys, ppr, lambda tt, pp, ppk: p.op(
                    "act", lambda e: e.copy(out=qT[:, tt * 512:(tt + 1) * 512], in_=pp[:]), reads=[ppk], writes=[K(("qT", tt))]))
                yield
                w, wk = ld(28 + h)
                proj_fm(ph, w, wk, xn, xkeys, ppr, lambda tt, pp, ppk: p.op(
                    "dve", lambda e: e.tensor_copy(out=kT[:, tt * 512:(tt + 1) * 512], in_=pp[:]), reads=[ppk], writes=[K(("kT", tt))]))
                proj_tm(ph, w, wk, xn, xkeys, ppr, lambda t4, pp, ppk: p.op(
                    "act", lambda e: e.copy(out=kt[:, t4 * 4:(t4 + 1) * 4, :], in_=pp[:].rearrange("p (a b) -> p a b", a=4)),
                    reads=[ppk], writes=[K(("kt", t4))]))
                yield
                for vc in range(2):
                    w, wk = ld(32 + 2 * h + vc)
                    proj_tm(ph, w, wk, xn, xkeys, ppr, lambda t4, pp, ppk, vc=vc: p.op(
                        "dve", lambda e: e.tensor_copy(out=vt[:, t4 * 4:(t4 + 1) * 4, vc * 128:(vc + 1) * 128],
                                                       in_=pp[:].rearrange("p (a b) -> p a b", a=4)),
                        reads=[ppk, K(("vt", t4))], writes=[K(("vt", t4))]))
                    yield
                for vc in range(2):
                    w, wk = ld(40 + 2 * h + vc)
                    proj_fm(ph, w, wk, xn, xkeys, ppr, lambda tt, pp, ppk, vc=vc: p.op(
                        "act", lambda e: e.activation(out=sr[:, vc, tt * 512:(tt + 1) * 512], in_=pp[:], func=AF.Silu),
                        reads=[ppk], writes=[K(("sr", vc, tt))]))
                    yield
                if STOP <= 2:
                    return
                for t4 in range(4):
                    pp, ppk = ppr.next()

                    def mmx(e, pp=pp, t4=t4, h=h):
                        for i in range(4):
                            t16 = t4 * 4 + i
                            ins = e.matmul(pp[:, i * 128:(i + 1) * 128], glr[:, t16 * 128:(t16 + 1) * 128], w2[:, h * 128:(h + 1) * 128],
                                           start=True, stop=True)
                        return ins
                    p.op("pe", mmx, reads=["glr", "w2"], writes=[ppk])
                    t1, t1k = t1r.next()
                    p.op("act", lambda e, t1=t1, pp=pp: e.activation(out=t1[:], in_=pp[:], func=AF.Exp, scale=-1.0), reads=[ppk], writes=[t1k])
                    p.op("act", lambda e, t1=t1, t4=t4: e.activation(out=spt[:, t4 * 4:(t4 + 1) * 4, :].rearrange("p a b -> p (a b)"), in_=t1[:],
                                                                   func=AF.Ln, bias=1.0, scale=1.0), reads=[t1k], writes=[K(("spt", t4))])
                    yield
                if STOP <= 3:
                    return
                p.op("pool", lambda e: e.memset(S[:], 0.0), reads=[K("S")], writes=[K("S")])
                p.op("pool", lambda e: e.memset(Sb[:], 0.0), reads=[K("Sb")], writes=[K("Sb")])
                for n in range(16):
                    cs = slice(n * 128, (n + 1) * 128)
                    tt = n // 4
                    p.op("pe", lambda e, n=n: e.matmul(pa[:, 0, :], spt[:, n, :], LE, start=True, stop=True),
                         reads=[K(("spt", n // 4)), "cst"], writes=[K("pa")])
                    p.op("pe", lambda e, n=n: e.matmul(pa[:, 1, :], GT, spt[:, n, :], start=True, stop=True),
                         reads=[K(("spt", n // 4)), "cst"], writes=[K("pa")])
                    E, Ek = Er.next()
                    p.op("act", lambda e, E=E: e.activation(out=E[:, 0, :], in_=pa[:, 0, :], func=AF.Exp, scale=-1.0 / 16), reads=[K("pa")], writes=[(Ek, 0)])
                    p.op("act", lambda e, E=E: e.activation(out=E[:, 1, :], in_=pa[:, 0, :], func=AF.Exp, scale=1.0 / 16), reads=[K("pa")], writes=[(Ek, 1)])
                    dk_, dkk = dkr.next()
                    p.op("act", lambda e, dk_=dk_: e.activation(out=dk_[:], in_=pa[:, 1, :], func=AF.Exp, scale=-1.0 / 16), reads=[K("pa")], writes=[dkk])
                    yield
                    if CUT <= 1:
                        return
                    qd, qdk = qdr.next()
                    ki, kik = kir.next()
                    kd, kdk = kdr.next()
                    p.op("dve", lambda e, qd=qd, E=E, cs=cs: e.scalar_tensor_tensor(out=qd[:], in0=qT[:, cs], scalar=qscale, in1=E[:, 0, :],
                                                                                     op0=ALU.mult, op1=ALU.mult),
                         reads=[K(("qT", tt)), (Ek, 0)], writes=[qdk])
                    p.op("dve", lambda e, ki=ki, E=E, cs=cs: e.tensor_tensor(out=ki[:], in0=kT[:, cs], in1=E[:, 1, :], op=ALU.mult),
                         reads=[K(("kT", tt)), (Ek, 1)], writes=[kik])
                    p.op("pool", lambda e, kd=kd, dk_=dk_, n=n: e.tensor_tensor(out=kd[:], in0=kt[:, n, :], in1=dk_[:], op=ALU.mult),
                         reads=[K(("kt", n // 4)), dkk], writes=[kdk])
                    p.op("pe", lambda e, ki=ki, qd=qd: e.matmul(pa[:, 2, :], ki[:], qd[:], start=True, stop=True), reads=[kik, qdk], writes=[K("pa")])
                    yield
                    if CUT <= 2:
                        return
                    ST, STk = STr.next()
                    p.op("dve", lambda e, ST=ST: e.tensor_tensor(out=ST[:], in0=pa[:, 2, :], in1=LE, op=ALU.mult), reads=[K("pa"), "cst"], writes=[STk])

                    def mmo(e, ST=ST, qd=qd, n=n):
                        for vc in range(2):
                            e.matmul(pb[:, vc * 128:(vc + 1) * 128], vt[:, n, vc * 128:(vc + 1) * 128], ST[:], start=True, stop=False)
                            ins = e.matmul(pb[:, vc * 128:(vc + 1) * 128], Sb[:, vc * 128:(vc + 1) * 128], qd[:], start=False, stop=True)
                        return ins
                    p.op("pe", mmo, reads=[STk, qdk, K(("vt", n // 4)), K("Sb")], writes=[K("pb")])
                    p.op("pe", lambda e, kd=kd, n=n: e.matmul(pb[:, 256:512], kd[:], vt[:, n, :], start=True, stop=True),
                         reads=[kdk, K(("vt", n // 4))], writes=[K("pb")])
                    yield
                    if CUT <= 3:
                        return
                    p.op("act", lambda e, cs=cs: e.copy(out=og[:, :, cs], in_=pb[:, 0:256].rearrange("p (a b) -> p a b", a=2)),
                         reads=[K("pb")], writes=[K(("og", n))])
                    p.op("dve", lambda e, E=E: e.scalar_tensor_tensor(out=S[:], in0=S[:], scalar=E[:, 0, 127:128], in1=pb[:, 256:512],
                                                                      op0=ALU.mult, op1=ALU.add),
                         reads=[K("S"), (Ek, 0), K("pb")], writes=[K("S")])
                    p.op("act", lambda e: e.copy(out=Sb[:], in_=S[:]), reads=[K("S")], writes=[K("Sb")])
                    yield
                    if CUT <= 4:
                        return
                    if CUT == 5 and n == 1:
                        return
                if STOP <= 4:
                    return
                for tt in range(4):
                    ts = slice(tt * 512, (tt + 1) * 512)
                    okeys = [K(("og", n)) for n in range(tt * 4, tt * 4 + 4)]
                    p.op("act", lambda e, ts=ts: e.activation(out=sq[:], in_=og[:, :, ts], func=AF.Square), reads=okeys, writes=[K("sq")])

                    def mms(e):
                        e.matmul(pss[:], ones[:], sq[:, 0, :], start=True, stop=False)
                        return e.matmul(pss[:], ones[:], sq[:, 1, :], start=False, stop=True)
                    p.op("pe", mms, reads=[K("sq"), "ones"], writes=["pss"])
                    rs, rsk = rsr.next()
                    rstd_from_sumsq(ph, pss, rs, "pss", rsk, 512, 1.0 / 256)
                    for vc in range(2):
                        o1, o1k = o1r.next()
                        ob, obk = obr.next()
                        p.op("dve", lambda e, o1=o1, rs=rs, vc=vc, ts=ts: e.scalar_tensor_tensor(
                            out=o1[:], in0=og[:, vc, ts], scalar=gn[:, vc:vc + 1], in1=rs[:], op0=ALU.mult, op1=ALU.mult),
                            reads=okeys + [rsk, "gn"], writes=[o1k])
                        p.op("pool", lambda e, o1=o1, ob=ob, vc=vc, ts=ts: e.tensor_tensor(out=ob[:], in0=o1[:], in1=sr[:, vc, ts], op=ALU.mult),
                             reads=[o1k, K(("sr", vc, tt))], writes=[obk])
                        p.dma("sp", oT[8 + 2 * h + vc, :, ts], ob[:], reads=[obk])
                    yield
                h = h + 1 if h + 1 < 4 else None

        interleave([head_stream(0, 0)])


def outproj_phase(nc, name, oT, yT, w_d, kc_n):
    with Phase(nc, name) as ph:
        p = ph.p
        o = ph.sb([128, kc_n, T], BF16, "o")
        keys = []
        for k4 in range(kc_n // 4):
            ks = slice(k4 * 4, k4 * 4 + 4)
            p.dma("sp", o[:, ks, :], oT[ks].rearrange("k p t -> p k t"), writes=[("o", k4)])
            keys.append(("o", k4))
        wr = Ring(ph, 2, [128, kc_n, 128], BF16, "w")
        ppr = Ring(ph, 2, [128, 512], F32, "pp", psum=True)
        out_proj(ph, o, keys, kc_n, w_d, yT, wr, ppr)


def gdn_prep_phase(nc, name, xnT, win, wba, cw_d, alog_d, dtb_d, gc_d, S):
    qscale = float(128 ** -0.5)
    with Phase(nc, name) as ph:
        p = ph.p
        make_eps(ph)
        xn, xkeys = load_xn(ph, xnT)
        cst = ph.sb([128, 7, 128], F32, "cst")
        p.dma("sp", cst[:], gc_d, writes=["cst"])
        ident, BDLE, BDGT, CA, CB = cst[:, 0, :], cst[:, 1, :], cst[:, 2, :], cst[:, 4, :], cst[:, 5, :]
        ones = ph.sb([128, 128], BF16, "ones")
        p.op("pool", lambda e: e.memset(ones[:], 1.0), writes=["ones"])
        wr = Ring(ph, 3, [128, KC, 128], BF16, "w")
        ppr = Ring(ph, 2, [128, 512], F32, "pp", psum=True)
        ptr_ = Ring(ph, 2, [128, 512], F32, "pt", psum=True)
        pss = ph.ps([128, 512], F32, "pss")
        pgg = ph.ps([128, 4, 32], F32, "pgg")
        wb = ph.sb([128, KC, 64], BF16, "wb")
        p.dma("pool", wb[:], wba, writes=["wb"])
        alog = ph.sb([128, 4, 32], F32, "alog")
        dtb = ph.sb([128, 4, 32], F32, "dtb")
        p.dma("sp", alog[:], alog_d, writes=["alog"])
        p.dma("sp", dtb[:], dtb_d, writes=["dtb"])
        nea = ph.sb([128, 4, 32], F32, "nea")
        p.op("act", lambda e: e.activation(out=nea[:], in_=alog[:], func=AF.Exp), reads=["alog"], writes=["nea"])
        p.op("dve", lambda e: e.tensor_scalar(out=nea[:], in0=nea[:], scalar1=-1.0, scalar2=None, op0=ALU.mult), reads=["nea"], writes=["nea"])
        beta = ph.sb([128, 16, 32], F32, "beta")
        nbeta = ph.sb([128, 16, 32], F32, "nbeta")
        gt = ph.sb([128, 16, 32], F32, "gt")
        e4 = ph.sb([128, 4, 16, 32], F32, "e4")
        tmr = Ring(ph, 2, [128, 4, 32], F32, "tm")

        def ba_evac(t4, pp, ppk):
            v = pp[:, 0:256].rearrange("p (a b) -> p a b", a=4)
            t4s = slice(t4 * 4, (t4 + 1) * 4)
            t1, t1k = tmr.next()
            p.op("act", lambda e: e.activation(out=t1[:], in_=v[:, :, 0:32], func=AF.Exp, scale=-1.0), reads=[ppk], writes=[t1k])
            p.op("dve", lambda e: e.tensor_scalar(out=t1[:], in0=t1[:], scalar1=1.0, scalar2=None, op0=ALU.add), reads=[t1k], writes=[t1k])
            p.op("dve", lambda e: e.reciprocal(out=beta[:, t4s, :], in_=t1[:]), reads=[t1k], writes=[("beta", t4)])
            p.op("pool", lambda e: e.tensor_scalar(out=nbeta[:, t4s, :], in0=beta[:, t4s, :], scalar1=-1.0, scalar2=None, op0=ALU.mult),
                 reads=[("beta", t4)], writes=[("nbeta", t4)])
            t2, t2k = tmr.next()
            p.op("dve", lambda e: e.tensor_tensor(out=t2[:], in0=v[:, :, 32:64], in1=dtb[:], op=ALU.add), reads=[ppk, "dtb"], writes=[t2k])
            p.op("act", lambda e: e.activation(out=t2[:], in_=t2[:], func=AF.Exp), reads=[t2k], writes=[t2k])
            p.op("act", lambda e: e.activation(out=t2[:], in_=t2[:], func=AF.Ln, bias=1.0, scale=1.0), reads=[t2k], writes=[t2k])
            p.op("dve", lambda e: e.tensor_tensor(out=gt[:, t4s, :], in0=t2[:], in1=nea[:], op=ALU.mult), reads=[t2k, "nea"], writes=[("gt", t4)])
        proj_tm(ph, wb, "wb", xn, xkeys, ppr, ba_evac, n=64)
        for n in range(16):
            def mmg(e, n=n):
                e.matmul(pgg[:, 0, :], BDLE, gt[:, n, :], start=True, stop=True)
                e.matmul(pgg[:, 1, :], BDGT, gt[:, n, :], start=True, stop=True)
                e.matmul(pgg[:, 2, :], CA, gt[:, n, :], start=True, stop=True)
                return e.matmul(pgg[:, 3, :], CB, gt[:, n, :], start=True, stop=True)
            p.op("pe", mmg, reads=[("gt", n // 4), "cst"], writes=["pgg"])
            p.op("act", lambda e, n=n: e.activation(out=e4[:, :, n, :], in_=pgg[:], func=AF.Exp), reads=["pgg"], writes=[("e4", n)])
        p.dma("sp", S["sc"][0], beta[:], reads=[("beta", i) for i in range(4)])
        p.dma("sp", S["sc"][1], nbeta[:], reads=[("nbeta", i) for i in range(4)])
        for j in range(4):
            p.dma("sp", S["sc"][2 + j], e4[:, j], reads=[("e4", n) for n in range(16)])
        p.dma("sp", S["sc"][6], gt[:], reads=[("gt", i) for i in range(4)])
        ekm = ph.sb([128, 2, 16, 32], F32, "ekm")
        for x, cm in enumerate((CA, CB)):
            p.op("dve", lambda e, x=x, cm=cm: e.tensor_scalar(out=ekm[:, x], in0=e4[:, 1], scalar1=cm[:, 0:1], scalar2=None, op0=ALU.mult),
                 reads=[("e4", n) for n in range(16)] + ["cst"], writes=[("ekm", x)])
            p.dma("sp", S["sc"][7 + x], ekm[:, x], reads=[("ekm", x)])

        NS = 2
        xc = [ph.sb([128, 3 + T], F32, "xc") for _ in range(NS)]
        acc = [ph.sb([128, T], F32, "acc") for _ in range(NS)]
        cs = [ph.sb([128, T], F32, "cs") for _ in range(NS)]
        sq = [ph.sb([128, T], BF16, "sq") for _ in range(NS)]
        rs = [ph.sb([128, T], F32, "rs") for _ in range(NS)]
        csb = [ph.sb([128, T], BF16, "csb") for _ in range(NS)]
        tokb = [ph.sb([128, 16, 128], BF16, "tokb") for _ in range(NS)]
        for i in range(NS):
            p.op("pool", lambda e, i=i: e.memset(xc[i][:, 0:3], 0.0), writes=[("xc0", i)])
        cwr = Ring(ph, 2, [128, 4], F32, "cw")
        slot = [0]

        def conv_tile(tile):
            s_ = slot[0] % NS
            slot[0] += 1
            w, wk = wr.next()
            p.dma("pool", w[:], win[tile], writes=[wk])
            cw, cwk = cwr.next()
            p.dma("sp", cw[:], cw_d[tile], writes=[cwk])
            proj_fm(ph, w, wk, xn, xkeys, ppr, lambda tt, pp, ppk: p.op(
                "act", lambda e: e.copy(out=xc[s_][:, 3 + tt * 512:3 + (tt + 1) * 512], in_=pp[:]), reads=[ppk], writes=[("xc", s_, tt)]))
            xk = [("xc", s_, tt) for tt in range(4)] + [("xc0", s_)]
            p.op("dve", lambda e: e.tensor_scalar(out=acc[s_][:], in0=xc[s_][:, 0:T], scalar1=cw[:, 0:1], scalar2=None, op0=ALU.mult),
                 reads=xk + [cwk], writes=[("acc", s_)])
            for i in range(1, 4):
                p.op("dve", lambda e, i=i: e.scalar_tensor_tensor(out=acc[s_][:], in0=xc[s_][:, i:i + T], scalar=cw[:, i:i + 1], in1=acc[s_][:],
                                                                 op0=ALU.mult, op1=ALU.add), reads=xk + [cwk, ("acc", s_)], writes=[("acc", s_)])
            p.op("act", lambda e: e.activation(out=cs[s_][:], in_=acc[s_][:], func=AF.Silu), reads=[("acc", s_)], writes=[("cs", s_)])
            return s_

        def l2n(s_, scl):
            p.op("act", lambda e: e.activation(out=sq[s_][:], in_=cs[s_][:], func=AF.Square), reads=[("cs", s_)], writes=[("sq", s_)])
            for tt in range(4):
                ts = slice(tt * 512, (tt + 1) * 512)
                p.op("pe", lambda e, ts=ts: e.matmul(pss[:], ones[:], sq[s_][:, ts], start=True, stop=True), reads=[("sq", s_), "ones"], writes=["pss"])
                p.op("act", lambda e, ts=ts: e.activation(out=rs[s_][:, ts], in_=pss[:], func=AF.Sqrt, bias=ph.eps_ap[:, 0:1], scale=1.0),
                     reads=["pss", "eps"], writes=[("rs", s_, tt)])
            rk = [("rs", s_, tt) for tt in range(4)]
            p.op("dve", lambda e: e.reciprocal(out=rs[s_][:], in_=rs[s_][:]), reads=rk, writes=rk)
            p.op("dve", lambda e: e.scalar_tensor_tensor(out=cs[s_][:], in0=cs[s_][:], scalar=scl, in1=rs[s_][:], op0=ALU.mult, op1=ALU.mult),
                 reads=[("cs", s_)] + rk, writes=[("cs", s_)])

        def to_tok(s_):
            dst = tokb[s_]
            for t4 in range(4):
                pt, ptk = ptr_.next()

                def mmt(e, pt=pt, t4=t4):
                    for i in range(4):
                        t16 = t4 * 4 + i
                        ins = e.transpose(pt[:, i * 128:(i + 1) * 128], cs[s_][:, t16 * 128:(t16 + 1) * 128], ident)
                    return ins
                p.op("pe", mmt, reads=[("cs", s_), "cst"], writes=[ptk])
                if t4 % 2 == 0:
                    p.op("act", lambda e, pt=pt, t4=t4: e.copy(out=dst[:, t4 * 4:(t4 + 1) * 4, :], in_=pt[:].rearrange("p (a b) -> p a b", a=4)),
                         reads=[ptk], writes=[("tokb", s_, t4)])
                else:
                    p.op("dve", lambda e, pt=pt, t4=t4: e.tensor_copy(out=dst[:, t4 * 4:(t4 + 1) * 4, :], in_=pt[:].rearrange("p (a b) -> p a b", a=4)),
                         reads=[ptk], writes=[("tokb", s_, t4)])
            return [("tokb", s_, i) for i in range(4)]

        def store_fm(s_, dst):
            p.op("act", lambda e: e.copy(out=csb[s_][:], in_=cs[s_][:]), reads=[("cs", s_)], writes=[("csb", s_)])
            p.dma("sp", dst, csb[s_][:], reads=[("csb", s_)])

        jobs = []
        for hk in range(16):
            jobs.append(("q", hk, hk))
            jobs.append(("k", hk, 16 + hk))
        for hv in range(32):
            jobs.append(("v", hv, 32 + hv))

        def finish(job, s_):
            kind, idx, _ = job
            if kind == "q":
                l2n(s_, qscale)
                store_fm(s_, S["qk"][idx, 0])
            elif kind == "k":
                l2n(s_, 1.0)
                store_fm(s_, S["qk"][idx, 1])
                keys = to_tok(s_)
                p.dma("sp", S["ktok"][idx], tokb[s_][:], reads=keys)
            else:
                keys = to_tok(s_)
                p.dma("sp", S["vtok"][idx], tokb[s_][:], reads=keys)
        prev = None
        for job in jobs:
            s_ = conv_tile(job[2])
            if prev is not None:
                finish(*prev)
            prev = (job, s_)
        finish(*prev)
        ztr = Ring(ph, 2, [128, 16, 128], BF16, "zt")
        for hv in range(32):
            w, wk = wr.next()
            p.dma("pool", w[:], win[64 + hv], writes=[wk])
            zt, ztk = ztr.next()
            proj_tm(ph, w, wk, xn, xkeys, ppr, lambda t4, pp, ppk, zt=zt, ztk=ztk: p.op(
                "act", lambda e: e.activation(out=zt[:, t4 * 4:(t4 + 1) * 4, :], in_=pp[:].rearrange("p (a b) -> p a b", a=4), func=AF.Silu),
                reads=[ppk], writes=[(ztk, t4)]))
            p.dma("sp", S["ztok"][hv], zt[:], reads=[(ztk, i) for i in range(4)])


def gdn_core_phase(nc, name, oT, gnrow_d, gc_d, S, nstreams=4):
    with Phase(nc, name) as ph:
        p = ph.p
        make_eps(ph)
        cst = ph.sb([128, 7, 128], F32, "cst")
        p.dma("sp", cst[:], gc_d, writes=["cst"])
        ident, BDLE, BDGT, BDLT, CA, CB, ONES = [cst[:, i, :] for i in range(7)]
        identb = ph.sb([128, 128], BF16, "identb")
        p.op("act", lambda e: e.copy(out=identb[:], in_=ident), reads=["cst"], writes=["identb"])
        onesb = ph.sb([128, 128], BF16, "onesb")
        p.op("act", lambda e: e.copy(out=onesb[:], in_=ONES), reads=["cst"], writes=["onesb"])
        gnrow = ph.sb([128, 128], F32, "gnrow")
        p.dma("sp", gnrow[:], gnrow_d, writes=["gnrow"])
        sc = ph.sb([128, 8, 16, 32], F32, "sc")
        for j, src in enumerate((0, 1, 2, 3, 4, 5, 7, 8)):
            p.dma("sp", sc[:, j], S["sc"][src], writes=[("sc", j)])
        sck = [("sc", j) for j in range(8)]
        BETA, NBETA, EGC, EKD, ELA, ELB, EKDA, EKDB = range(8)

        def stream(sid):
            K = lambda s: (s, sid)
            qT = ph.sb([128, T], BF16, "qT")
            kT = ph.sb([128, T], BF16, "kT")
            ktok = ph.sb([128, 16, 128], BF16, "ktok")
            vtok = ph.sb([128, 16, 128], BF16, "vtok")
            ztok = ph.sb([128, 16, 128], BF16, "ztok")
            oTs = ph.sb([128, T], BF16, "oTs")
            Sst = ph.sb([128, 128], F32, "S")
            Sb = ph.sb([128, 128], BF16, "Sb")
            pA = ph.ps([128, 4, 128], F32, "pA")
            pB = pA
            pC = ph.ps([128, 512], F32, "pC")
            pD = pC.rearrange("p (a b) -> p a b", a=4)
            mk = lambda nm, shp, dt: Ring(ph, 2, shp, dt, f"{nm}{sid}")
            GMr, edr, dsr, dir_ = mk("GM", [128, 128], F32), mk("ed", [128, 128], F32), mk("ds", [128, 128], F32), mk("di", [128, 128], F32)
            Yr, YTr, Mr = mk("Y", [128, 128], F32), mk("YT", [128, 128], F32), mk("M", [128, 128], F32)
            QKr, Mbr, Rr = mk("QK", [128, 128], BF16), mk("Mb", [128, 128], BF16), mk("kg", [128, 128], BF16)
            dgr, bcr = mk("dg", [128, 256], BF16), mk("bc", [128, 128], F32)
            ur, wTr, qpr, kdr = mk("u", [128, 128], F32), mk("wT", [128, 128], BF16), mk("qp", [128, 128], BF16), mk("kd", [128, 2, 128], BF16)
            vnr, osr, smr, ogr, otr = mk("vn", [128, 128], BF16), mk("o", [128, 128], F32), mk("sm", [128, 4], F32), mk("og", [128, 128], BF16), mk("ot", [128, 128], F32)
            for hv in range(sid, 32, nstreams):
                hk = hv // 2
                p.dma("sp", qT[:], S["qk"][hk, 0], writes=[K("qT")])
                p.dma("sp", kT[:], S["qk"][hk, 1], writes=[K("kT")])
                p.dma("sp", ktok[:], S["ktok"][hk], writes=[K("ktok")])
                p.dma("sp", vtok[:], S["vtok"][hv], writes=[K("vtok")])
                p.dma("sp", ztok[:], S["ztok"][hv], writes=[K("ztok")])
                p.op("pool", lambda e: e.memset(Sst[:], 0.0), reads=[K("S")], writes=[K("S")])
                p.op("pool", lambda e: e.memset(Sb[:], 0.0), reads=[K("Sb")], writes=[K("Sb")])
                for n in range(16):
                    cs = slice(n * 128, (n + 1) * 128)
                    col = lambda j, n=n, hv=hv: sc[:, j, n, hv:hv + 1]
                    GM, GMk = GMr.next()
                    p.op("dve", lambda e, GM=GM, n=n, hv=hv: e.tensor_scalar(out=GM[:], in0=BDGT, scalar1=S_g[:, n, hv:hv + 1], scalar2=None, op0=ALU.mult),
                         reads=["cst", "gtl"], writes=[GMk])
                    p.op("pe", lambda e, GM=GM: e.matmul(pA[:, 2, :], GM[:], BDLE, start=True, stop=True), reads=[GMk, "cst"], writes=[K("pA")])
                    p.op("pe", lambda e, cs=cs: e.matmul(pA[:, 0, :], kT[:, cs], kT[:, cs], start=True, stop=True), reads=[K("kT")], writes=[K("pA")])
                    p.op("pe", lambda e, cs=cs: e.matmul(pA[:, 1, :], kT[:, cs], qT[:, cs], start=True, stop=True), reads=[K("kT"), K("qT")], writes=[K("pA")])
                    ed, edk = edr.next()
                    p.op("act", lambda e, ed=ed: e.activation(out=ed[:], in_=pA[:, 2, :], func=AF.Exp), reads=[K("pA")], writes=[edk])
                    ds, dsk = dsr.next()
                    di, dik = dir_.next()
                    p.op("dve", lambda e, ds=ds, ed=ed: e.tensor_tensor(out=ds[:], in0=ed[:], in1=BDLT, op=ALU.mult), reads=[edk, "cst"], writes=[dsk])
                    p.op("pool", lambda e, di=di, ed=ed: e.tensor_tensor(out=di[:], in0=ed[:], in1=BDLE, op=ALU.mult), reads=[edk, "cst"], writes=[dik])
                    yield
                    Y, Yk = Yr.next()
                    p.op("dve", lambda e, Y=Y, ds=ds, col=col: e.scalar_tensor_tensor(out=Y[:], in0=pA[:, 0, :], scalar=col(NBETA), in1=ds[:],
                                                                                     op0=ALU.mult, op1=ALU.mult),
                         reads=[K("pA"), dsk] + sck, writes=[Yk])
                    QK, QKk = QKr.next()
                    p.op("dve", lambda e, QK=QK, di=di: e.tensor_tensor(out=QK[:], in0=pA[:, 1, :], in1=di[:], op=ALU.mult),
                         reads=[K("pA"), dik], writes=[QKk])
                    p.op("pe", lambda e, Y=Y: e.transpose(pA[:, 3, :], Y[:], ident), reads=[Yk, "cst"], writes=[K("pA")])
                    YT, YTk = YTr.next()
                    p.op("act", lambda e, YT=YT: e.copy(out=YT[:], in_=pA[:, 3, :]), reads=[K("pA")], writes=[YTk])
                    M, Mk = Mr.next()
                    p.op("pool", lambda e, M=M, Y=Y: e.tensor_tensor(out=M[:], in0=Y[:], in1=ident, op=ALU.add), reads=[Yk, "cst"], writes=[Mk])
                    yield
                    for it in range(5):
                        last = (it == 4)
                        if not last:
                            p.op("pe", lambda e, Y=Y, YT=YT: e.matmul(pB[:, 0, :], YT[:], Y[:], start=True, stop=True), reads=[Yk, YTk], writes=[K("pA")])
                        p.op("pe", lambda e, Y=Y, YT=YT: e.matmul(pB[:, 1, :], Y[:], YT[:], start=True, stop=True), reads=[Yk, YTk], writes=[K("pA")])
                        Y2, Y2k = Yr.next()
                        YT2, YT2k = YTr.next()
                        if not last:
                            p.op("dve", lambda e, Y2=Y2: e.tensor_copy(out=Y2[:], in_=pB[:, 0, :]), reads=[K("pA")], writes=[Y2k])
                        p.op("act", lambda e, YT2=YT2: e.copy(out=YT2[:], in_=pB[:, 1, :]), reads=[K("pA")], writes=[YT2k])
                        yield
                        p.op("pe", lambda e, YT2=YT2, M=M: e.matmul(pB[:, 2, :], YT2[:], M[:], start=True, stop=True), reads=[YT2k, Mk], writes=[K("pA")])
                        M2, M2k = Mr.next()
                        p.op("dve", lambda e, M2=M2, M=M: e.tensor_tensor(out=M2[:], in0=pB[:, 2, :], in1=M[:], op=ALU.add), reads=[K("pA"), Mk], writes=[M2k])
                        Y, Yk, YT, YTk, M, Mk = Y2, Y2k, YT2, YT2k, M2, M2k
                        yield
                    Mb, Mbk = Mbr.next()
                    p.op("act", lambda e, Mb=Mb, M=M: e.copy(out=Mb[:], in_=M[:]), reads=[Mk], writes=[Mbk])
                    R, Rk = Rr.next()
                    p.op("dve", lambda e, R=R, n=n, col=col: e.tensor_scalar(out=R[:], in0=ktok[:, n, :], scalar1=col(EGC), scalar2=None, op0=ALU.mult),
                         reads=[K("ktok")] + sck, writes=[Rk])
                    dg, dgk = dgr.next()
                    p.op("act", lambda e, dg=dg, col=col: e.mul(out=dg[:, 0:128], in_=ident, mul=col(BETA)), reads=["cst"] + sck, writes=[(dgk, 0)])
                    p.op("act", lambda e, dg=dg, col=col: e.mul(out=dg[:, 128:256], in_=ident, mul=col(EGC)), reads=["cst"] + sck, writes=[(dgk, 1)])
                    kd, kdk = kdr.next()
                    p.op("act", lambda e, kd=kd, n=n, col=col: e.mul(out=kd[:, 0, :], in_=ktok[:, n, :], mul=col(EKDA)), reads=[K("ktok")] + sck, writes=[(kdk, 0)])
                    p.op("pool", lambda e, kd=kd, n=n, col=col: e.tensor_scalar(out=kd[:, 1, :], in0=ktok[:, n, :], scalar1=col(EKDB), scalar2=None,
                                                                               op0=ALU.mult), reads=[K("ktok")] + sck, writes=[(kdk, 1)])
                    p.op("pe", lambda e, dg=dg: e.matmul(pC[:, 256:512], onesb[:], dg[:], start=True, stop=True), reads=[(dgk, 0), (dgk, 1), "onesb"], writes=[K("pC")])

                    def mmxw(e, Mb=Mb, R=R, n=n):
                        e.matmul(pC[:, 0:128], Mb[:], vtok[:, n, :], start=True, stop=True)
                        return e.matmul(pC[:, 128:256], Mb[:], R[:], start=True, stop=True)
                    p.op("pe", mmxw, reads=[Mbk, Rk, K("vtok")], writes=[K("pC")])
                    p.op("pe", lambda e, Mb=Mb, R=R: e.matmul(pB[:, 3, :], R[:], Mb[:], start=True, stop=True), reads=[Mbk, Rk], writes=[K("pA")])
                    yield
                    bc, bck = bcr.next()
                    p.op("act", lambda e, bc=bc: e.copy(out=bc[:], in_=pC[:, 256:384]), reads=[K("pC")], writes=[bck])
                    qp, qpk = qpr.next()
                    p.op("dve", lambda e, qp=qp, cs=cs: e.tensor_tensor(out=qp[:], in0=qT[:, cs], in1=pC[:, 384:512], op=ALU.mult), reads=[K("qT"), K("pC")], writes=[qpk])
                    u, uk = ur.next()
                    p.op("dve", lambda e, u=u, col=col: e.tensor_scalar(out=u[:], in0=pC[:, 0:128], scalar1=col(BETA), scalar2=None, op0=ALU.mult),
                         reads=[K("pC")] + sck, writes=[uk])
                    wT, wTk = wTr.next()
                    p.op("dve", lambda e, wT=wT, bc=bc: e.tensor_tensor(out=wT[:], in0=pB[:, 3, :], in1=bc[:], op=ALU.mult), reads=[K("pA"), bck], writes=[wTk])
                    yield
                    vn, vnk = vnr.next()
                    o, ok_ = osr.next()
                    for x in range(2):
                        xs = slice(64 * x, 64 * x + 64)
                        p.op("pe", lambda e, wT=wT: e.matmul(pD[:, 0, :], wT[:], Sb[:], start=True, stop=True), reads=[wTk, K("Sb")], writes=[K("pC")])
                        if x == 0:
                            p.op("dve", lambda e, vn=vn, u=u: e.tensor_tensor(out=vn[:], in0=u[:], in1=pD[:, 0, :], op=ALU.subtract), reads=[uk, K("pC")], writes=[vnk])
                        else:
                            p.op("dve", lambda e, vn=vn, u=u: e.tensor_tensor(out=vn[64:128, :], in0=u[64:128, :], in1=pD[64:128, 0, :], op=ALU.subtract),
                                 reads=[uk, K("pC"), vnk], writes=[vnk])
                        yield

                        def mmo(e, qp=qp, QK=QK, vn=vn):
                            e.matmul(pD[:, 1, :], qp[:], Sb[:], start=True, stop=False)
                            return e.matmul(pD[:, 1, :], QK[:], vn[:], start=False, stop=True)
                        p.op("pe", mmo, reads=[qpk, QKk, vnk, K("Sb")], writes=[K("pC")])
                        p.op("pe", lambda e, kd=kd, vn=vn, x=x: e.matmul(pD[:, 2, :], kd[:, x, :], vn[:], start=True, stop=True), reads=[(kdk, x), vnk], writes=[K("pC")])
                        p.op("act", lambda e, o=o, xs=xs: e.copy(out=o[xs, :], in_=pD[xs, 1, :]), reads=[K("pC"), ok_], writes=[ok_])
                        p.op("dve", lambda e, col=col, x=x: e.scalar_tensor_tensor(out=Sst[:], in0=Sst[:], scalar=col(ELA + x), in1=pD[:, 2, :],
                                                                                   op0=ALU.mult, op1=ALU.add), reads=[K("S"), K("pC")] + sck, writes=[K("S")])
                        p.op("act", lambda e: e.copy(out=Sb[:], in_=Sst[:]), reads=[K("S")], writes=[K("Sb")])
                        yield
                    sm, smk = smr.next()
                    ot, otk = otr.next()
                    p.op("act", lambda e, ot=ot, o=o, sm=sm: e.activation(out=ot[:], in_=o[:], func=AF.Square, accum_out=sm[:, 0:1]), reads=[ok_], writes=[otk, (smk, 0)])
                    p.op("act", lambda e, sm=sm: e.activation(out=sm[:, 1:2], in_=sm[:, 0:1], func=AF.Sqrt, bias=ph.eps_ap[:, 0:1], scale=1.0 / 128),
                         reads=[(smk, 0), "eps"], writes=[(smk, 1)])
                    p.op("dve", lambda e, sm=sm: e.reciprocal(out=sm[:, 2:3], in_=sm[:, 1:2]), reads=[(smk, 1)], writes=[(smk, 2)])
                    p.op("dve", lambda e, ot=ot, o=o, sm=sm: e.scalar_tensor_tensor(out=ot[:], in0=o[:], scalar=sm[:, 2:3], in1=gnrow[:], op0=ALU.mult, op1=ALU.mult),
                         reads=[ok_, (smk, 2), "gnrow", otk], writes=[otk])
                    og, ogk = ogr.next()
                    p.op("pool", lambda e, og=og, ot=ot, n=n: e.tensor_tensor(out=og[:], in0=ot[:], in1=ztok[:, n, :], op=ALU.mult), reads=[otk, K("ztok")], writes=[ogk])
                    p.op("pe", lambda e, og=og: e.matmul(pD[:, 3, :], og[:], identb[:], start=True, stop=True), reads=[ogk, "identb"], writes=[K("pC")])
                    p.op("act", lambda e, cs=cs: e.copy(out=oTs[:, cs], in_=pD[:, 3, :]), reads=[K("pC")], writes=[K(("oTs", n))])
                    yield
                p.dma("sp", oT[hv], oTs[:], reads=[K(("oTs", n)) for n in range(16)])

        S_g = ph.sb([128, 16, 32], F32, "gtl")
        p.dma("sp", S_g[:], S["sc"][6], writes=["gtl"])
        interleave([stream(s) for s in range(nstreams)])
```
